# Optimizing a Trainium2 kernel written in Bass

```python
import math
import jax, jax.numpy as jnp
from jax import lax
import numpy as np

D_MODEL = 1024
BATCH = 8
SEQ = 2048
DEPTH = 1

DIFF_HEADS = 4
DIFF_HEAD_DIM = 64
DIFF_V_DIM = 2 * DIFF_HEAD_DIM
DIFF_QK_W = DIFF_HEADS * 2 * DIFF_HEAD_DIM
DIFF_V_W = DIFF_HEADS * DIFF_V_DIM
RET_HEADS = 4
RET_QK_DIM = 128
RET_V_DIM = 128
RET_QK_W = RET_HEADS * RET_QK_DIM
RET_V_W = RET_HEADS * RET_V_DIM
RET_CHUNK = 128
Q_BLOCK = 128
ROPE_THETA = 10000.0
NORM_EPS = 1e-6
D_MIX = DIFF_V_W + RET_V_W
IN_WIDTHS = [DIFF_QK_W, DIFF_QK_W, DIFF_V_W, DIFF_V_W, RET_QK_W, RET_QK_W, RET_V_W, RET_V_W]
D_IN_PROJ = sum(IN_WIDTHS)
IN_SPLITS = [int(v) for v in np.cumsum(IN_WIDTHS)[:-1]]

kernel_name = "hybrid_diffattn_retention_block"


def rmsnorm(x, w=None, eps=NORM_EPS):
    xf = x.astype(jnp.float32)
    y = xf * lax.rsqrt(jnp.mean(xf * xf, axis=-1, keepdims=True) + eps)
    if w is not None:
        y = y * w.astype(jnp.float32)
    return y.astype(x.dtype)


def rotary(x, positions, theta=ROPE_THETA):
    d = x.shape[-1]
    half = d // 2
    inv_freq = theta ** (-jnp.arange(half, dtype=jnp.float32) / half)
    ang = positions.astype(jnp.float32)[..., None] * inv_freq
    ang = ang.reshape(ang.shape[:2] + (1,) * (x.ndim - 3) + (half,))
    cos, sin = jnp.cos(ang), jnp.sin(ang)
    xf = x.astype(jnp.float32)
    x1, x2 = xf[..., :half], xf[..., half:]
    out = jnp.concatenate([x1 * cos - x2 * sin, x2 * cos + x1 * sin], axis=-1)
    return out.astype(x.dtype)


def diff_attention(q, k, v, positions, lam):
    B, S, _ = q.shape
    nb = S // Q_BLOCK
    q = rotary(q.reshape(B, S, DIFF_HEADS, 2, DIFF_HEAD_DIM), positions)
    k = rotary(k.reshape(B, S, DIFF_HEADS, 2, DIFF_HEAD_DIM), positions)
    scale = DIFF_HEAD_DIM ** -0.5
    qf = (q.astype(jnp.float32) * scale).transpose(0, 2, 3, 1, 4)
    q_blocks = jnp.moveaxis(qf.reshape(B, DIFF_HEADS, 2, nb, Q_BLOCK, DIFF_HEAD_DIM), 3, 0)
    kf = k.astype(jnp.float32).transpose(0, 2, 3, 1, 4)
    vf = v.astype(jnp.float32).reshape(B, S, DIFF_HEADS, DIFF_V_DIM).transpose(0, 2, 1, 3)
    key_pos = jnp.arange(S)

    def one_block(args):
        qb, blk = args
        s = jnp.einsum('bhcqd,bhckd->bhcqk', qb, kf)
        q_pos = blk * Q_BLOCK + jnp.arange(Q_BLOCK)
        causal = key_pos[None, :] <= q_pos[:, None]
        s = jnp.where(causal, s, jnp.float32(-1e30))
        p = jax.nn.softmax(s, axis=-1)
        a = p[:, :, 0] - lam * p[:, :, 1]
        return jnp.einsum('bhqk,bhkv->bhqv', a, vf)

    out = lax.map(one_block, (q_blocks, jnp.arange(nb)))
    out = out.transpose(1, 0, 3, 2, 4).reshape(B, S, DIFF_HEADS, DIFF_V_DIM)
    return out.astype(v.dtype)


def retention_chunkwise(q, k, v, positions):
    B, S, _ = q.shape
    n = S // RET_CHUNK
    C = RET_CHUNK
    q = rotary(q.reshape(B, S, RET_HEADS, RET_QK_DIM), positions)
    k = rotary(k.reshape(B, S, RET_HEADS, RET_QK_DIM), positions)

    def chunks(t, d):
        return t.astype(jnp.float32).reshape(B, n, C, RET_HEADS, d).transpose(0, 3, 1, 2, 4)

    qc = chunks(q, RET_QK_DIM)
    kc = chunks(k, RET_QK_DIM) * (RET_QK_DIM ** -0.5)
    vc = chunks(v.reshape(B, S, RET_HEADS, RET_V_DIM), RET_V_DIM)

    log_g = jnp.log(1.0 - 2.0 ** (-5.0 - jnp.arange(RET_HEADS, dtype=jnp.float32)))
    idx = jnp.arange(C, dtype=jnp.float32)
    diff = idx[:, None] - idx[None, :]
    decay_in = jnp.where(diff >= 0, jnp.exp(jnp.maximum(diff, 0.0)[None] * log_g[:, None, None]), 0.0)
    zeta = jnp.exp((C - 1 - idx)[None] * log_g[:, None])
    xi = jnp.exp((idx + 1)[None] * log_g[:, None])
    g_chunk = jnp.exp(C * log_g)

    s = jnp.einsum('bhncd,bhnmd->bhncm', qc, kc) * decay_in[None, :, None]
    inner = jnp.einsum('bhncm,bhnme->bhnce', s, vc)
    kv = jnp.einsum('bhncd,bhnce->bhnde', kc * zeta[None, :, None, :, None], vc)

    def step(R, kv_n):
        return g_chunk[None, :, None, None] * R + kv_n, R

    R0 = jnp.zeros((B, RET_HEADS, RET_QK_DIM, RET_V_DIM), jnp.float32)
    _, R_prev = lax.scan(step, R0, jnp.moveaxis(kv, 2, 0))
    R_prev = jnp.moveaxis(R_prev, 0, 2)
    cross = jnp.einsum('bhncd,bhnde->bhnce', qc * xi[None, :, None, :, None], R_prev)
    out = (inner + cross).transpose(0, 2, 3, 1, 4).reshape(B, S, RET_HEADS, RET_V_DIM)
    return out.astype(v.dtype)


def setup_inputs(seed: int = 0) -> dict:
    key = jax.random.key(seed)
    ks = jax.random.split(key, 12)
    x = jax.random.normal(ks[0], (BATCH, SEQ, D_MODEL), jnp.float32)
    offsets = jax.random.randint(ks[1], (BATCH, 1), 0, 4096, dtype=jnp.int32)
    positions = (offsets + jnp.arange(SEQ, dtype=jnp.int32)[None, :]).astype(jnp.int32)
    norm_pre_w = 1.0 + 0.01 * jax.random.normal(ks[2], (DEPTH, D_MODEL), jnp.float32)
    w_in = jax.random.normal(ks[3], (DEPTH, D_MODEL, D_IN_PROJ), jnp.float32) * D_MODEL ** -0.5
    lambda_q1 = 0.1 * jax.random.normal(ks[4], (DEPTH, DIFF_HEAD_DIM), jnp.float32)
    lambda_k1 = 0.1 * jax.random.normal(ks[5], (DEPTH, DIFF_HEAD_DIM), jnp.float32)
    lambda_q2 = 0.1 * jax.random.normal(ks[6], (DEPTH, DIFF_HEAD_DIM), jnp.float32)
    lambda_k2 = 0.1 * jax.random.normal(ks[7], (DEPTH, DIFF_HEAD_DIM), jnp.float32)
    diff_subln_w = 1.0 + 0.01 * jax.random.normal(ks[8], (DEPTH, DIFF_V_DIM), jnp.float32)
    w_out = jax.random.normal(ks[9], (DEPTH, D_MIX, D_MODEL), jnp.float32) * D_MIX ** -0.5
    norm_post_w = 1.0 + 0.01 * jax.random.normal(ks[10], (DEPTH, D_MODEL), jnp.float32)
    return {"x": x, "positions": positions, "norm_pre_w": norm_pre_w, "w_in": w_in,
            "lambda_q1": lambda_q1, "lambda_k1": lambda_k1, "lambda_q2": lambda_q2, "lambda_k2": lambda_k2,
            "diff_subln_w": diff_subln_w, "w_out": w_out, "norm_post_w": norm_post_w}


def reference(x, positions, norm_pre_w, w_in, lambda_q1, lambda_k1, lambda_q2, lambda_k2,
              diff_subln_w, w_out, norm_post_w):
    B, S, _ = x.shape
    h = x
    for layer in range(DEPTH):
        lambda_init = 0.8 - 0.6 * math.exp(-0.3 * layer)
        u = rmsnorm(h, norm_pre_w[layer])
        proj = jnp.einsum('bsd,de->bse', u, w_in[layer])
        dq, dk, dv, dg, rq, rk, rv, rg = jnp.split(proj, IN_SPLITS, axis=-1)

        lam = (jnp.exp(jnp.sum(lambda_q1[layer].astype(jnp.float32) * lambda_k1[layer].astype(jnp.float32)))
               - jnp.exp(jnp.sum(lambda_q2[layer].astype(jnp.float32) * lambda_k2[layer].astype(jnp.float32)))
               + lambda_init)
        y_a = diff_attention(dq, dk, dv, positions, lam)
        y_a = rmsnorm(y_a, diff_subln_w[layer]) * (1.0 - lambda_init)
        y_a = y_a.reshape(B, S, DIFF_V_W) * jax.nn.silu(dg)

        y_b = retention_chunkwise(rq, rk, rv, positions)
        y_b = rmsnorm(y_b).reshape(B, S, RET_V_W) * jax.nn.silu(rg)

        y = jnp.einsum('bsm,md->bsd', jnp.concatenate([y_a, y_b], axis=-1), w_out[layer])
        h = h + rmsnorm(y, norm_post_w[layer])
    return h
```

```python
import math
import numpy as np
import concourse.bass as bass
import concourse.mybir as mybir
from concourse.bass_utils import run_bass_kernel_spmd

F32 = mybir.dt.float32
BF16 = mybir.dt.bfloat16
I32 = mybir.dt.int32
AF = mybir.ActivationFunctionType
ALU = mybir.AluOpType
AX = mybir.AxisListType

S_LEN = 2048
D = 1024
NT = 16
EPS = 1e-6
LAMBDA_INIT = 0.8 - 0.6 * math.exp(0.0)
QB = 256
NQB = S_LEN // QB
MAGIC = 12582912.0
TWO_PI = 2.0 * math.pi

STREAMS = ("pe", "act", "dve", "pool", "sp")
EMBED_WAIT = True
RELAXED_SAME_ENGINE = False


class _Op:
    __slots__ = ("idx", "stream", "dma", "fn", "key", "deps", "signaled", "count")


def _k(r):
    return r if isinstance(r, tuple) else (r,)


class _FirstInsWait:
    def __init__(self, eng, sem, val):
        self._e, self._sem, self._val, self._done = eng, sem, val, False

    def __getattr__(self, name):
        f = getattr(self._e, name)

        def call(*a, **k):
            ins = f(*a, **k)
            if not self._done:
                self._done = True
                ins._wait_ge(self._sem, self._val)
            return ins
        return call


class Sched:
    def __init__(self, nc):
        self.nc = nc
        self.ops = []
        self.last_writer = {}
        self.readers = {}
        self.children = {}

    def _related(self, key):
        rel = [key[:i] for i in range(1, len(key))]
        rel.append(key)
        rel.extend(self.children.get(key, ()))
        return rel

    def _register(self, key):
        if key in self.last_writer or key in self.readers:
            return
        for i in range(1, len(key)):
            self.children.setdefault(key[:i], set()).add(key)

    def add(self, stream, fn, reads=(), writes=(), dma=False, key=None):
        op = _Op()
        op.idx = len(self.ops)
        op.stream = stream
        op.dma = dma
        op.fn = fn
        op.key = (key if key is not None else ("dma", op.idx)) if dma else stream
        op.signaled = False
        op.count = 0
        reads = [_k(r) for r in reads]
        writes = [_k(r) for r in writes]
        deps = {}
        for r in reads:
            for q in self._related(r):
                w = self.last_writer.get(q)
                if w is not None:
                    deps[w] = "raw"
        for r in writes:
            pr = stream != "pe" and r[0] in ("PA", "PB", "PC", "PD")
            for q in self._related(r):
                w = self.last_writer.get(q)
                if w is not None and (w not in deps or (deps[w] == "pr" and not pr)):
                    deps[w] = "pr" if pr else "waw"
                for rd in self.readers.get(q, ()):
                    if rd not in deps or (deps[rd] == "pr" and not pr):
                        deps[rd] = "pr" if pr else "war"
        for r in reads:
            self._register(r)
            self.readers.setdefault(r, []).append(op.idx)
        for r in writes:
            self._register(r)
            self.last_writer[r] = op.idx
            self.readers[r] = []
            for q in self.children.get(r, ()):
                self.readers[q] = []
        op.deps = deps
        self.ops.append(op)
        return op

    def emit(self):
        nc = self.nc
        ops = self.ops
        for op in ops:
            keep = {}
            for d, kind in op.deps.items():
                if d == op.idx:
                    continue
                dop = ops[d]
                if not dop.dma and not op.dma and dop.stream == op.stream:
                    if op.stream == "pe" or kind == "pr" or (RELAXED_SAME_ENGINE and kind != "raw"):
                        continue
                if dop.dma:
                    keep[("d", d)] = d
                else:
                    k = ("c", dop.stream)
                    if k not in keep or keep[k] < d:
                        keep[k] = d
            op.deps = sorted(keep.values())
            for d in op.deps:
                ops[d].signaled = True
        counters = {}
        for op in ops:
            if op.signaled:
                counters[op.key] = counters.get(op.key, 0) + (16 if op.dma else 1)
                op.count = counters[op.key]
        sems = {k: nc.alloc_semaphore(name="s%d" % i) for i, k in enumerate(counters.keys())}
        by_stream = {s: [o for o in ops if o.stream == s] for s in STREAMS}

        known = {sn: {} for sn in STREAMS}
        vc = {}
        needs = {}

        def merge(dst, src):
            for k, v in src.items():
                if dst.get(k, 0) < v:
                    dst[k] = v

        for op in ops:
            kn = known[op.stream]
            need = []
            for d in op.deps:
                dop = ops[d]
                if kn.get(dop.key, 0) >= dop.count:
                    continue
                need.append(dop)
                kn[dop.key] = dop.count
                merge(kn, vc[d])
            needs[op.idx] = need
            v = dict(kn)
            if op.signaled:
                v[op.key] = op.count
                if not op.dma:
                    kn[op.key] = max(kn.get(op.key, 0), 0)
            vc[op.idx] = v

        def run_stream(eng, sname):
            for op in by_stream[sname]:
                need = list(needs[op.idx])
                emb = need.pop() if (need and EMBED_WAIT and not op.dma and sname in ("act", "dve", "pool", "pe")
                                     and not getattr(op.fn, "noembed", False)) else None
                for dop in need:
                    eng.wait_ge(sems[dop.key], dop.count)
                if emb is not None:
                    ins = op.fn(_FirstInsWait(eng, sems[emb.key], emb.count))
                else:
                    ins = op.fn(eng)
                if op.signaled:
                    ins.then_inc(sems[op.key], 16 if op.dma else 1)

        with nc.Block() as block:
            @block.sync
            def _(e):
                run_stream(e, "sp")

            @block.scalar
            def _(e):
                run_stream(e, "act")

            @block.vector
            def _(e):
                run_stream(e, "dve")

            @block.gpsimd
            def _(e):
                run_stream(e, "pool")

            @block.tensor
            def _(e):
                run_stream(e, "pe")


def _single(f):
    f.single = True
    return f


def TT(out, in0, in1, op):
    return _single(lambda e: e.tensor_tensor(out=out, in0=in0, in1=in1, op=op))


def TS(out, in0, s1, op0, s2=None, op1=None):
    if op1 is None:
        return _single(lambda e: e.tensor_scalar(out=out, in0=in0, scalar1=s1, scalar2=None, op0=op0))
    return _single(lambda e: e.tensor_scalar(out=out, in0=in0, scalar1=s1, scalar2=s2, op0=op0, op1=op1))


def STT(out, in0, scalar, in1, op0, op1):
    return _single(lambda e: e.scalar_tensor_tensor(out=out, in0=in0, scalar=scalar, in1=in1, op0=op0, op1=op1))


def ACTV(out, in_, func, scale=1.0, accum_out=None):
    if accum_out is None:
        return lambda e: e.activation(out=out, in_=in_, func=func, scale=scale)
    return lambda e: e.activation(out=out, in_=in_, func=func, scale=scale, accum_out=accum_out)


def CP(out, in_):
    return _single(lambda e: e.tensor_copy(out=out, in_=in_))


def DMA(out, in_, **kw):
    return lambda e: e.dma_start(out=out, in_=in_, **kw)


def bc(ap, axis, shape):
    return ap.unsqueeze(axis).broadcast_to(shape)


def build_program():
    nc = bass.Bass("TRN2", target_bir_lowering=False)

    def din(name, shape, dt=F32):
        return nc.dram_tensor(name, shape, dt, kind="ExternalInput").ap()

    x_d = din("x", [S_LEN, D])
    pos_d = din("pos", [128, NT], I32)
    win_d = din("w_in", [4, 128, 8, 1024])
    wout_d = din("w_out", [128, 8, 1024])
    npre_d = din("npre", [128, D])
    npost_d = din("npost", [128, D])
    subw_d = din("subw", [128, 128])
    lq_d = din("lq", [128, 4, 64])
    ident_d = din("ident", [128, 128])
    tri_d = din("tri", [128, 128])
    maskr_d = din("maskr", [128, 512])
    invf_d = din("invf", [128, 64])
    cr_d = din("cr", [128, 16])
    out_d = nc.dram_tensor("out", [S_LEN, D], F32, kind="ExternalOutput").ap()

    def sb(name, shape, dt):
        return nc.alloc_sbuf_tensor("s_" + name, shape, dt)

    U = sb("U", [128, 16384], BF16)
    uT = U[:, :].rearrange("p (c s) -> p c s", c=8)
    wout = U[:, 0:8192].rearrange("p (c n) -> p c n", c=8)
    NPT = 4
    PT = [U[:, 8192 + i * 512: 8192 + (i + 1) * 512].rearrange("p (c q) -> p c q", c=2) for i in range(NPT)]
    NSTG = 2
    stg = [[U[:, 10240 + (b * 3 + i) * 512: 10240 + (b * 3 + i + 1) * 512].bitcast(F32)
            .rearrange("p (j d) -> p j d", j=2) for i in range(3)] for b in range(NSTG)]

    NQZ = 4
    Qz = [U[:, 13312 + i * 512: 13312 + (i + 1) * 512] for i in range(NQZ)]
    Wb = [sb("Wb%d" % i, [128, 8192], BF16) for i in range(2)]
    Wv = [w[:, :].rearrange("p (c n) -> p c n", c=8) for w in Wb]
    W1f = Wb[1][:, :].bitcast(F32)
    ang = W1f[:, 0:1024].rearrange("p (t j) -> p t j", t=NT)
    kf = W1f[:, 1024:2048].rearrange("p (t j) -> p t j", t=NT)
    rr = W1f[:, 2048:3072].rearrange("p (t j) -> p t j", t=NT)
    identf = W1f[:, 3072:3200]
    trif = W1f[:, 3200:3328]

    Q = sb("Q", [128, 16384], BF16)
    QKT = Q[:, :].rearrange("p (a h s) -> p a h s", a=2, h=4)
    Qf = Q[:, :].bitcast(F32)
    ostg = [Qf[:, i * 1024:(i + 1) * 1024] for i in range(2)]
    NXR = 4
    xre = [Qf[:, 2048 + i * 1024: 2048 + (i + 1) * 1024] for i in range(NXR)]

    V = sb("V", [128, NT, 4, 129], BF16)
    SG = sb("SG", [128, NT, 512], BF16)

    Yt = sb("Ytok", [128, NT * 1024], BF16)
    Ytok = Yt[:, :].rearrange("p (t f) -> p t f", t=NT)

    def ydc(t):
        return Ytok[:, t, 0:512]
    QKTc = [ydc(i).rearrange("p (a h s) -> p a h s", a=2, h=2) for i in range(2)]

    def half(k):
        return ydc(2 + k // 2)[:, (k % 2) * 256:(k % 2 + 1) * 256]
    Ktokc = [half(i).rearrange("p (h d) -> p h d", h=2) for i in range(0, 3)]
    Vc = [half(i).rearrange("p (h d) -> p h d", h=2) for i in range(3, 6)]
    SGc = [half(i) for i in (6, 7, 8, 13)]
    Aq = [half(i) for i in range(9, 11)]
    Rbf = [half(i).rearrange("p (h d) -> p h d", h=2) for i in range(11, 13)]
    zc = [ydc(9 + i).bitcast(F32) for i in range(3)]
    NXB = 4
    xb = [Ytok[:, 4 * i:4 * i + 4, 512:1024].bitcast(F32) for i in range(NXB)]

    def x4(ap2d):
        return ap2d.rearrange("p (a b) -> p a b", a=4)

    Ctab = sb("Ctab", [128, NT, 64], F32)
    Stab = sb("Stab", [128, NT, 64], F32)
    NUB = 5
    ub = [sb("ub%d" % i, [128, 1024], BF16) for i in range(NUB)]
    junk = sb("junk", [128, 1024], BF16)
    junk2 = sb("junk2", [128, 1024], BF16)
    jk = {"n": 0}

    def junk_full():
        jk["n"] += 1
        b = jk["n"] % 2
        return (junk, junk2)[b], ("junk", "f", b)

    def junk_128():
        jk["n"] += 1
        k = jk["n"] % 8
        return junk[:, k * 128:(k + 1) * 128], ("junk", "s", k)
    NW = sb("NW", [128, 1024], F32)
    R = sb("R", [128, 2, 128], F32)
    t1 = sb("t1", [128, 512], F32)
    t2 = sb("t2", [128, 512], F32)
    qkrot = [sb("qkrot%d" % i, [128, 512], BF16) for i in range(2)]
    thb = [sb("thb%d" % i, [128, 256], F32) for i in range(2)]
    sgs = sb("sgs", [128, 256], F32)
    identb = sb("identb", [128, 128], BF16)
    trib = sb("trib", [128, 128], BF16)
    maskr = sb("maskr", [128, 512], F32)
    CW = sb("CW", [128, 4, 128], F32)
    subw = sb("subw", [128, 128], F32)
    invf = sb("invf", [128, 64], F32)
    posi = sb("posi", [128, NT], I32)
    posf = sb("posf", [128, NT], F32)
    CR = sb("CR", [128, 16], F32)
    MH = sb("MH", [128, 16], F32)
    LQ = sb("LQ", [128, 4, 64], F32)
    lqp = sb("lqp", [128, 2, 64], F32)
    lst = sb("lst", [128, 8], F32)
    ssA = sb("ssA", [128, NT], F32)
    tmA = sb("tmA", [128, NT], F32)
    rsA = sb("rsA", [128, NT], F32)
    ssR = sb("ssR", [128, 2 * NT, 2], F32)
    tmR = sb("tmR", [128, 2 * NT, 2], F32)
    rsR = sb("rsR", [128, 2 * NT, 2], F32)
    scR = sb("scR", [128, 2 * NT, 2], F32)
    NNRM = NQB * 4
    rlD = sb("rlD", [128, NNRM, 2, 2], F32)
    rl1n = sb("rl1n", [128, NNRM, 2], F32)
    ssD = sb("ssD", [128, NNRM, 2], F32)
    tmD = sb("tmD", [128, NNRM, 2], F32)
    rsD = sb("rsD", [128, NNRM, 2], F32)
    ssO = sb("ssO", [128, NT], F32)
    tmO = sb("tmO", [128, NT], F32)
    rsO = sb("rsO", [128, NT], F32)

    PS = {n: nc.alloc_psum_tensor("ps" + n, [128, 1024], F32) for n in "ABCD"}

    def bank(n, b):
        return PS[n][:, b * 512:(b + 1) * 512]

    def bank_bf(n, b):
        return bank(n, b).bitcast(BF16)

    S = Sched(nc)
    A = S.add

    def load_w(pair, buf):
        A("pool", DMA(Wv[buf][:, :, :], win_d[pair], max_dma_last_dim=8192), writes=[("W", buf)], dma=True, key=("W", buf))

    load_w(0, 0)
    A("sp", DMA(posi[:, :], pos_d[:, :]), writes=["posi"], dma=True, key="c_pos")
    A("sp", DMA(invf[:, :], invf_d[:, :]), writes=["invf"], dma=True, key="c_invf")
    A("sp", DMA(identf, ident_d[:, :]), writes=[("W", 1, "identf")], dma=True, key="c_id")
    A("sp", DMA(xb[0], x4(x_d[0:128, :])), writes=[("YR", "xb", 0)], dma=True, key=("xb", 0))
    A("sp", DMA(NW[:, :], npre_d[:, :]), writes=["NW"], dma=True, key="c_nw")
    for i in range(1, NXB):
        A("sp", DMA(xb[i], x4(x_d[i * 128:(i + 1) * 128, :])), writes=[("YR", "xb", i)], dma=True, key=("xb", i))
    A("sp", DMA(CR[:, :], cr_d[:, :]), writes=["CR"], dma=True, key="c_cr")
    A("sp", DMA(maskr[:, :], maskr_d[:, :]), writes=["maskr"], dma=True, key="c_maskr")
    A("sp", DMA(trif, tri_d[:, :]), writes=[("W", 1, "trif")], dma=True, key="c_tri")
    A("sp", DMA(subw[:, :], subw_d[:, :]), writes=["subw"], dma=True, key="c_subw")
    A("sp", DMA(LQ[:, :, :], lq_d[:, :, :]), writes=["LQ"], dma=True, key="c_lq")

    A("dve", lambda e: e.memset(MH[:, :], -0.5), writes=["MH"])
    A("dve", CP(identb[:, :], identf), reads=[("W", 1, "identf")], writes=["identb"])
    A("dve", CP(posf[:, :], posi[:, :]), reads=["posi"], writes=["posf"])
    A("dve", TT(ang, bc(posf[:, :], 2, [128, NT, 64]), bc(invf[:, :], 1, [128, NT, 64]), ALU.mult),
      reads=["posf", "invf"], writes=[("W", 1, "ang")])
    c1 = 6.28125
    c2 = float(np.float32(TWO_PI - c1))
    c3 = float(TWO_PI - c1 - c2)
    PI_LO = 3.1415925
    PERIOD_LO = float(np.nextafter(np.float32(TWO_PI), np.float32(0.0)))
    A("dve", TS(kf, ang, 1.0 / TWO_PI, ALU.mult, MAGIC, ALU.add), reads=[("W", 1, "ang")], writes=[("W", 1, "kf")])
    A("dve", TS(kf, kf, -MAGIC, ALU.add), reads=[("W", 1, "kf")], writes=[("W", 1, "kf")])
    A("dve", STT(rr, kf, -c1, ang, ALU.mult, ALU.add), reads=[("W", 1, "kf"), ("W", 1, "ang")], writes=[("W", 1, "rr")])
    A("dve", STT(rr, kf, -c2, rr, ALU.mult, ALU.add), reads=[("W", 1, "kf"), ("W", 1, "rr")], writes=[("W", 1, "rr")])
    A("dve", TS(rr, rr, -PI_LO, ALU.max, PI_LO, ALU.min), reads=[("W", 1, "rr")], writes=[("W", 1, "rr")])
    A("act", ACTV(Stab[:, :, :], rr, AF.Sin), reads=[("W", 1, "rr")], writes=["Stab"])
    A("dve", TS(kf, rr, math.pi / 2.0, ALU.add, PI_LO, ALU.is_gt), reads=[("W", 1, "rr")], writes=[("W", 1, "kf")])
    A("dve", STT(kf, kf, -PERIOD_LO, rr, ALU.mult, ALU.add), reads=[("W", 1, "kf"), ("W", 1, "rr")], writes=[("W", 1, "kf")])
    A("act", lambda e: e.activation(out=Ctab[:, :, :], in_=kf, func=AF.Sin, bias=math.pi / 2.0, scale=1.0),
      reads=[("W", 1, "kf")], writes=["Ctab"])
    neglam = lst[:, 5:6]

    def phaseA_s1(t):
        b = t % NUB
        xi = t % NXB
        jt, jkey = junk_full()
        A("act", ACTV(x4(jt[:, :]), xb[xi], AF.Square, accum_out=ssA[:, t:t + 1]),
          reads=[("YR", "xb", xi)], writes=[jkey, ("ssA", t)])
        A("dve", TS(tmA[:, t:t + 1], ssA[:, t:t + 1], 1.0 / D, ALU.mult, EPS, ALU.add),
          reads=[("ssA", t)], writes=[("tmA", t)])
        A("pool", TT(rsA[:, t:t + 1], tmA[:, t:t + 1], MH[:, 0:1], ALU.pow),
          reads=[("tmA", t), "MH"], writes=[("rsA", t)])
        A("dve", STT(x4(ub[b][:, :]), xb[xi], rsA[:, t:t + 1], x4(NW[:, :]), ALU.mult, ALU.mult),
          reads=[("YR", "xb", xi), ("rsA", t), "NW"], writes=[("ub", b)])
        if t + NXB < NT:
            A("sp", DMA(xb[xi], x4(x_d[(t + NXB) * 128:(t + NXB + 1) * 128, :])), writes=[("YR", "xb", xi)],
              dma=True, key=("xb", xi))

    def phaseA_s2(t):
        b = t % NUB
        pb_ = t % 2
        pT = bank_bf("C", pb_)

        def tr8(e, pT=pT, b=b):
            ins = None
            for c in range(8):
                ins = e.transpose(out=pT[:, c * 128:(c + 1) * 128], in_=ub[b][:, c * 128:(c + 1) * 128],
                                  identity=identb[:, :])
            return ins
        A("pe", tr8, reads=[("ub", b), "identb"], writes=[("PC", pb_)])
        A("act", ACTV(uT[:, :, t * 128:(t + 1) * 128], pT.rearrange("p (c s) -> p c s", c=8), AF.Copy),
          writes=[("PC", pb_), ("U", "uT", t)])

    aq = {"s1": 0, "s2": 0}

    def phaseA_s1_step():
        if aq["s1"] < NT:
            phaseA_s1(aq["s1"])
            aq["s1"] += 1

    def phaseA_s2_step():
        if aq["s2"] < aq["s1"]:
            phaseA_s2(aq["s2"])
            aq["s2"] += 1
            if aq["s2"] == NT:
                A("sp", DMA(NW[:, :], npost_d[:, :]), writes=["NW"], dma=True, key="c_nw")

    def phaseA_step():
        phaseA_s2_step()
        phaseA_s1_step()

    for _ in range(NUB):
        phaseA_s1_step()
    phaseA_s2_step()
    phaseA_s1_step()
    phaseA_s2_step()
    A("dve", CP(trib[:, :], trif), reads=[("W", 1, "trif")], writes=["trib"])
    A("dve", TT(lqp[:, 0, :], LQ[:, 0, :], LQ[:, 1, :], ALU.mult), reads=["LQ"], writes=["lqp"])
    A("dve", TT(lqp[:, 1, :], LQ[:, 2, :], LQ[:, 3, :], ALU.mult), reads=["LQ", "lqp"], writes=["lqp"])
    A("dve", lambda e: e.tensor_reduce(out=lst[:, 0:2], in_=lqp[:, :, :], axis=AX.X, op=ALU.add),
      reads=["lqp"], writes=[("lst", 0)])
    A("act", ACTV(lst[:, 2:4], lst[:, 0:2], AF.Exp), reads=[("lst", 0)], writes=[("lst", 1)])
    A("dve", TT(lst[:, 4:5], lst[:, 3:4], lst[:, 2:3], ALU.subtract), reads=[("lst", 1)], writes=[("lst", 2)])
    A("dve", TS(lst[:, 5:6], lst[:, 4:5], -LAMBDA_INIT, ALU.add), reads=[("lst", 2)], writes=[("lst", 3)])
    A("dve", TS(CW[:, :, :], bc(subw[:, :], 1, [128, 4, 128]), (1.0 - LAMBDA_INIT) * 0.5, ALU.mult),
      reads=["subw"], writes=["CW"])

    A("dve", lambda e: e.memset(V[:, :, :, :], 1.0), writes=["V"])

    cnt = {"p": 0}

    def part1(wbuf, t, is_ret, pair, psn):
        i = cnt["p"]
        cnt["p"] += 1
        psP = PS[psn][:, :]
        pkey = ("P" + psn,)
        rb = i % 2
        cb = t % 2

        def mm16(e, psP=psP, wbuf=wbuf, t=t):
            ins = None
            for c in range(8):
                for bk in range(2):
                    ins = e.matmul(psP[:, bk * 512:(bk + 1) * 512], lhsT=uT[:, c, t * 128:(t + 1) * 128],
                                   rhs=Wv[wbuf][:, c, bk * 512:(bk + 1) * 512], start=(c == 0), stop=(c == 7))
            return ins
        A("pe", mm16, reads=[("U", "uT", t), ("W", wbuf)], writes=[pkey])
        gcol = psP[:, 768:1024]
        vcol = psP[:, 512:768].rearrange("p (h d) -> p h d", h=2)
        A("act", ACTV(thb[rb][:, :], gcol, AF.Tanh, scale=0.5), writes=[pkey, ("thb", rb)])
        if is_ret:
            A("act", ACTV(Vc[cb][:, :, :], vcol, AF.Copy), writes=[pkey, ("YD", "Vc", cb)])
            ng, hf = 4, 64
            ct = Ctab[:, t, :]
            st = Stab[:, t, :]
        else:
            A("act", ACTV(V[:, t, 2 * pair:2 * pair + 2, 0:128], vcol, AF.Copy), writes=[pkey, ("V", t, pair)])
            ng, hf = 8, 32
            ct = Ctab[:, t, :].rearrange("p (i two) -> p i two", two=2)[:, :, 0]
            st = Stab[:, t, :].rearrange("p (i two) -> p i two", two=2)[:, :, 0]
        pv = psP[:, 0:512].rearrange("p (g h i) -> p g h i", g=ng, h=2)
        t1v = t1[:, :].rearrange("p (g h i) -> p g h i", g=ng, h=2)
        t2v = t2[:, :].rearrange("p (g h i) -> p g h i", g=ng, h=2)
        cb4 = ct.unsqueeze(1).unsqueeze(1).broadcast_to([128, ng, 2, hf])
        sb3 = st.unsqueeze(1).broadcast_to([128, ng, hf])
        A("dve", TT(t1v, pv, cb4, ALU.mult), reads=["Ctab"], writes=[pkey, "t1"])
        A("dve", STT(t2v[:, :, 0, :], pv[:, :, 1, :], -1.0, sb3, ALU.mult, ALU.mult),
          reads=["Stab"], writes=[pkey, ("t2", 0)])
        A("dve", TT(t2v[:, :, 1, :], pv[:, :, 0, :], sb3, ALU.mult), reads=["Stab"], writes=[pkey, ("t2", 1)])
        if is_ret:
            A("dve", STT(SGc[cb], thb[rb][:, :], 1.0, gcol, ALU.add, ALU.mult),
              reads=[("thb", rb)], writes=[pkey, ("YD", "SGc", cb)])
        else:
            A("dve", STT(SG[:, t, 2 * pair * 128:(2 * pair + 2) * 128], thb[rb][:, :], 1.0, gcol, ALU.add, ALU.mult),
              reads=[("thb", rb)], writes=[pkey, ("SG", t, pair)])
        A("dve", TT(qkrot[rb][:, :], t1[:, :], t2[:, :], ALU.add), reads=["t1", "t2"], writes=[("qkrot", rb)])
        if is_ret:
            for hl in range(2):
                h = 2 * pair + hl
                A("act", ACTV(Ktokc[cb][:, hl, :], qkrot[rb][:, hl * 256 + 128:hl * 256 + 256], AF.Copy,
                              scale=CR[:, h:h + 1]),
                  reads=[("qkrot", rb), "CR"], writes=[("YD", "Ktokc", cb, hl)])
        return (i, rb, cb)

    def part2(t, is_ret, pair, tok):
        i, rb, cb = tok
        slot = i % 2
        pT2 = bank_bf("C", slot)[:, 0:512]
        tkey = ("PC", slot)

        def tr4(e, pT2=pT2, rb=rb):
            ins = None
            for j in range(4):
                ins = e.transpose(out=pT2[:, j * 128:(j + 1) * 128], in_=qkrot[rb][:, j * 128:(j + 1) * 128],
                                  identity=identb[:, :])
            return ins
        A("pe", tr4, reads=[("qkrot", rb), "identb"], writes=[tkey])
        for hl in range(2):
            src = pT2[:, hl * 256:(hl + 1) * 256].rearrange("p (a s) -> p a s", a=2)
            if is_ret:
                A("act", ACTV(QKTc[cb][:, :, hl, :], src, AF.Copy), writes=[tkey, ("YD", "QKTc", cb, hl)])
            else:
                A("act", ACTV(QKT[:, :, 2 * pair + hl, t * 128:(t + 1) * 128], src, AF.Copy),
                  writes=[tkey, ("Q", "qkt", 2 * pair + hl, t)])

    pend = None
    for pair in range(2):
        for t in range(NT):
            tok = part1(pair, t, False, pair, "ABD"[cnt["p"] % 3])
            if pend is not None:
                part2(*pend)
            pend = (t, False, pair, tok)
            phaseA_step()
            if pair == 0 and t == 6:
                load_w(1, 1)
        if pair == 0:
            load_w(2, 0)
    part2(*pend)
    pend = None
    load_w(3, 1)
    while aq["s2"] < NT:
        phaseA_step()

    A("sp", lambda e: e.nop(), writes=[("YD",), ("YR",)])

    SQS = (4.0 / 128.0) ** 0.5

    NCH = 2 * NT

    def ret_iter(i):
        n0, n1, n2, n3 = i, i - 1, i - 2, i - 3
        has0, has1, has2, has3 = (0 <= n0 < NCH), (0 <= n1 < NCH), (0 <= n2 < NCH), (0 <= n3 < NCH)
        psPn = "AB"[i % 2]
        psP = PS[psPn][:, :]
        pk0, pk1 = ("P" + psPn, 0), ("P" + psPn, 1)
        psS = bank("C", 0)[:, 0:256]
        skey = ("PC", 0)
        psKV = bank("C", 1)[:, 0:256]
        kkey = ("PC", 1)
        psO = bank("D", 0)[:, 0:256]
        okey = ("PD", 0)
        pT2 = bank_bf("D", 1)[:, 0:512]
        tkey = ("PD", 1)
        if has2:
            rp2, t2_ = divmod(n2, NT)
            cq2, c32 = n2 % 2, n2 % 3
            rbuf = n2 % 2

            def mmS(e, cq2=cq2):
                ins = None
                for hl in range(2):
                    ins = e.matmul(psS[:, hl * 128:(hl + 1) * 128], lhsT=QKTc[cq2][:, 1, hl, :], rhs=QKTc[cq2][:, 0, hl, :],
                                   start=True, stop=True)
                return ins
            A("pe", mmS, reads=[("YD", "QKTc", cq2)], writes=[skey])
            A("dve", TT(Aq[cq2], psS, maskr[:, 2 * rp2 * 128:(2 * rp2 + 2) * 128], ALU.mult), reads=["maskr"],
              writes=[skey, ("YD", "Aq", cq2)])
        if has0:
            rp0, t0 = divmod(n0, NT)
            rb, c30 = n0 % 2, n0 % 3

            def mm8(bk):
                def f(e, rp=rp0, t=t0, bk=bk):
                    ins = None
                    for c in range(8):
                        ins = e.matmul(psP[:, bk * 512:(bk + 1) * 512], lhsT=uT[:, c, t * 128:(t + 1) * 128],
                                       rhs=Wv[rp][:, c, bk * 512:(bk + 1) * 512], start=(c == 0), stop=(c == 7))
                    return ins
                return f
            A("pe", mm8(0), reads=[("U", "uT", t0), ("W", rp0)], writes=[pk0])
        if has1:
            rb1, cq1 = n1 % 2, n1 % 2

            def tr4(e, rb1=rb1):
                ins = None
                for j in range(4):
                    ins = e.transpose(out=pT2[:, j * 128:(j + 1) * 128], in_=qkrot[rb1][:, j * 128:(j + 1) * 128],
                                      identity=identb[:, :])
                return ins
            A("pe", tr4, reads=[("qkrot", rb1), "identb"], writes=[tkey])
            A("act", ACTV(QKTc[cq1].rearrange("p a h s -> p h a s"),
                          pT2.rearrange("p (h a s) -> p h a s", h=2, a=2), AF.Copy),
              writes=[tkey, ("YD", "QKTc", cq1)])
        if has0:
            A("pe", mm8(1), reads=[("U", "uT", t0), ("W", rp0)], writes=[pk1])
            gcol = psP[:, 768:1024]
            vcol = psP[:, 512:768].rearrange("p (h d) -> p h d", h=2)
            A("act", ACTV(thb[rb][:, :], gcol, AF.Tanh, scale=0.5), writes=[pk1, ("thb", rb)])
            A("act", ACTV(Vc[c30][:, :, :], vcol, AF.Copy), writes=[pk1, ("YD", "Vc", c30)])
        if has2:
            def mmO(e, cq2=cq2, c32=c32, t=t2_, rbuf=rbuf):
                ins = None
                for hl in range(2):
                    o = psO[:, hl * 128:(hl + 1) * 128]
                    ins = e.matmul(o, lhsT=Aq[cq2][:, hl * 128:(hl + 1) * 128], rhs=Vc[c32][:, hl, :],
                                   start=True, stop=(t == 0))
                    if t > 0:
                        ins = e.matmul(o, lhsT=QKTc[cq2][:, 0, hl, :], rhs=Rbf[rbuf][:, hl, :], start=False, stop=True)
                return ins
            A("pe", mmO, reads=[("YD", "Aq", cq2), ("YD", "Vc", c32), ("YD", "QKTc", cq2), ("YD", "Rbf", rbuf)],
              writes=[okey])
            if t2_ < NT - 1:
                def mmKV(e, c32=c32):
                    ins = None
                    for hl in range(2):
                        ins = e.matmul(psKV[:, hl * 128:(hl + 1) * 128], lhsT=Ktokc[c32][:, hl, :], rhs=Vc[c32][:, hl, :],
                                       start=True, stop=True)
                    return ins
                A("pe", mmKV, reads=[("YD", "Ktokc", c32), ("YD", "Vc", c32)], writes=[kkey])
            A("act", ACTV(zc[n2 % 3], psO, AF.Copy), writes=[okey, ("YD", "zc", n2 % 3)])
        if has0:
            ng, hf = 4, 64
            ct = Ctab[:, t0, :]
            st = Stab[:, t0, :]
            pv = psP[:, 0:512].rearrange("p (g h i) -> p g h i", g=ng, h=2)
            t1v = t1[:, :].rearrange("p (g h i) -> p g h i", g=ng, h=2)
            t2v = t2[:, :].rearrange("p (g h i) -> p g h i", g=ng, h=2)
            cb4 = ct.unsqueeze(1).unsqueeze(1).broadcast_to([128, ng, 2, hf])
            sb3 = st.unsqueeze(1).broadcast_to([128, ng, hf])
            A("dve", TT(t1v, pv, cb4, ALU.mult), reads=["Ctab"], writes=[pk0, "t1"])
            A("dve", STT(t2v[:, :, 0, :], pv[:, :, 1, :], -1.0, sb3, ALU.mult, ALU.mult),
              reads=["Stab"], writes=[pk0, ("t2", 0)])
            A("dve", TT(t2v[:, :, 1, :], pv[:, :, 0, :], sb3, ALU.mult), reads=["Stab"], writes=[pk0, ("t2", 1)])
            A("dve", STT(SGc[n0 % 4], thb[rb][:, :], 1.0, gcol, ALU.add, ALU.mult),
              reads=[("thb", rb)], writes=[pk1, ("YD", "SGc", n0 % 4)])
            A("dve", TT(qkrot[rb][:, :], t1[:, :], t2[:, :], ALU.add), reads=["t1", "t2"], writes=[("qkrot", rb)])
        if has2 and t2_ < NT - 1:
            kvv = psKV.rearrange("p (h d) -> p h d", h=2)
            if t2_ == 0:
                A("dve", CP(R[:, :, :], kvv), writes=[kkey, "R"])
            else:
                def rupd(e, kvv=kvv, rp=rp2):
                    ins = None
                    for hl in range(2):
                        h = 2 * rp + hl
                        ins = e.scalar_tensor_tensor(out=R[:, hl, :], in0=R[:, hl, :], scalar=CR[:, 12 + h:13 + h],
                                                     in1=kvv[:, hl, :], op0=ALU.mult, op1=ALU.add)
                    return ins
                A("dve", rupd, reads=["R", "CR"], writes=[kkey, "R"])
            A("pool", CP(Rbf[1 - rbuf][:, :, :], R[:, :, :]), reads=["R"], writes=[("YD", "Rbf", 1 - rbuf)])
        if has0:
            for hl in range(2):
                h = 2 * rp0 + hl
                A("act", ACTV(Ktokc[c30][:, hl, :], qkrot[rb][:, hl * 256 + 128:hl * 256 + 256], AF.Copy,
                              scale=CR[:, h:h + 1]),
                  reads=[("qkrot", rb), "CR"], writes=[("YD", "Ktokc", c30, hl)])
        if has3:
            rp3, t3_ = divmod(n3, NT)
            c0 = 512 + 2 * rp3 * 128

            def ytail(e, n3=n3, t3_=t3_, c0=c0):
                ins = None
                for hl in range(2):
                    ins = e.scalar_tensor_tensor(out=Ytok[:, t3_, c0 + hl * 128:c0 + (hl + 1) * 128],
                                                 in0=zc[n3 % 3][:, hl * 128:(hl + 1) * 128], scalar=rsR[:, n3, hl:hl + 1],
                                                 in1=SGc[n3 % 4][:, hl * 128:(hl + 1) * 128], op0=ALU.mult, op1=ALU.mult)
                return ins
            A("dve", ytail, reads=[("rsR", n3), ("YD", "SGc", n3 % 4), ("YD", "zc", n3 % 3)], writes=[("YR", "y", t3_, rp3)])
        if has2:
            for hl in range(2):
                js, jkey = junk_128()
                A("act", ACTV(js, zc[n2 % 3][:, hl * 128:(hl + 1) * 128], AF.Square, scale=SQS,
                              accum_out=ssR[:, n2, hl:hl + 1]),
                  reads=[("YD", "zc", n2 % 3)], writes=[jkey, ("ssR", n2, hl)])
            A("dve", TT(tmR[:, n2, :], ssR[:, n2, :], CR[:, 4 + 2 * rp2:6 + 2 * rp2], ALU.add), reads=[("ssR", n2), "CR"],
              writes=[("tmR", n2)])
            A("pool", TT(rsR[:, n2, :], tmR[:, n2, :], MH[:, 0:2], ALU.pow), reads=[("tmR", n2), "MH"],
              writes=[("rsR", n2)])

    for i in range(NCH + 3):
        ret_iter(i)

    A("sp", lambda e: e.nop(), writes=[("U",), ("YD",)])

    blocks = [(hb, qb) for hb in range(4) for qb in range(NQB)]
    steps = []
    for bi, (hb, qb) in enumerate(blocks):
        for kt in range(2 * qb + 2):
            steps.append((bi, hb, qb, kt))
    OB = ("A", "B")
    STB = (("C", 0), ("C", 1), ("D", 0), ("D", 1))
    for i in range(NQZ):
        A("dve", lambda e, i=i: e.memset(Qz[i], 0.0), writes=[("U", "qz", i)])

    def emit_qz(bi, eng="pool"):
        hb, qb = blocks[bi]
        z = Qz[bi % NQZ]
        A(eng, CP(z[0:64, 0:256], QKT[0:64, 0, hb, qb * QB:(qb + 1) * QB]), reads=[("Q", "qkt", hb)],
          writes=[("U", "qz", bi % NQZ, 0)])
        A(eng, CP(z[64:128, 256:512], QKT[64:128, 0, hb, qb * QB:(qb + 1) * QB]), reads=[("Q", "qkt", hb)],
          writes=[("U", "qz", bi % NQZ, 1)])

    def emit_qk(s):
        bi, hb, qb, kt = steps[s]
        off = max(0, kt - 2 * qb) * 128
        stn, stb = STB[s % 4]
        psST = bank(stn, stb)
        z = Qz[bi % NQZ]

        diag = kt >= 2 * qb

        def mmQK(e, psST=psST, hb=hb, kt=kt, off=off, z=z, diag=diag):
            kT = QKT[:, 1, hb, kt * 128:(kt + 1) * 128]
            if not diag:
                return e.matmul(psST[:, 0:512], lhsT=kT, rhs=z[:, 0:512], start=True, stop=True)
            if off == 0:
                e.matmul(psST[:, 0:512], lhsT=kT, rhs=z[:, 0:512], start=True, stop=False, skip_group_check=True)
                e.matmul(psST[:, 0:128], lhsT=identb[:, :], rhs=trib[:, :], start=False, stop=False,
                         skip_group_check=True)
                return e.matmul(psST[:, 256:384], lhsT=identb[:, :], rhs=trib[:, :], start=False, stop=True,
                                skip_group_check=True)
            e.matmul(psST[:, off:256], lhsT=kT, rhs=z[:, off:256], start=True, stop=False, skip_group_check=True)
            e.matmul(psST[:, off:256], lhsT=identb[:, :], rhs=trib[:, :], start=False, stop=False,
                     skip_group_check=True)
            e.matmul(psST[:, 256 + off:512], lhsT=kT, rhs=z[:, 256 + off:512], start=False, stop=False,
                     skip_group_check=True)
            return e.matmul(psST[:, 256 + off:512], lhsT=identb[:, :], rhs=trib[:, :], start=False, stop=True,
                            skip_group_check=True)
        A("pe", mmQK, reads=[("Q", "qkt", hb), ("U", "qz", bi % NQZ), "identb", "trib"], writes=[("P" + stn, stb)])

    def normalize_head(nrm, hb, qb, psOt, okey):
        sg = stg[nrm % NSTG]
        skeys = [("U", "stg", nrm % NSTG, i) for i in range(3)]
        n = nrm
        o4 = psOt[:, :, 0:258].rearrange("p c (j k) -> p c j k", k=129)
        A("dve", lambda e, o4=o4, n=n: e.reciprocal(out=rlD[:, n, :, :], in_=o4[:, :, :, 128]),
          writes=[okey, ("rlD", n)])
        A("dve", TS(rl1n[:, n, :], rlD[:, n, 1, :], neglam, ALU.mult), reads=[("rlD", n), ("lst", 3)],
          writes=[("rl1n", n)])
        A("dve", TT(sg[0], o4[:, 0, :, 0:128], bc(rlD[:, n, 0, :], 2, [128, 2, 128]), ALU.mult),
          reads=[("rlD", n)], writes=[okey, skeys[0]])
        def ya_op(e, o4=o4, sg=sg, n=n):
            ins = None
            for j in range(2):
                ins = e.scalar_tensor_tensor(out=sg[1][:, j, :], in0=o4[:, 1, j, 0:128], scalar=rl1n[:, n, j:j + 1],
                                             in1=sg[0][:, j, :], op0=ALU.mult, op1=ALU.add)
            return ins
        A("dve", ya_op, reads=[("rl1n", n), skeys[0]], writes=[okey, skeys[1]])
        A("dve", TT(sg[2], sg[1], sg[1], ALU.mult), reads=[skeys[1]], writes=[skeys[2]])
        A("dve", lambda e, sg=sg, n=n: e.tensor_reduce(out=ssD[:, n, :], in_=sg[2], axis=AX.X, op=ALU.add),
          reads=[skeys[2]], writes=[("ssD", n)])
        A("dve", TS(tmD[:, n, :], ssD[:, n, :], 1.0 / 128.0, ALU.mult, EPS, ALU.add), reads=[("ssD", n)],
          writes=[("tmD", n)])
        A("pool", TT(rsD[:, n, :], tmD[:, n, :], MH[:, 0:2], ALU.pow), reads=[("tmD", n), "MH"], writes=[("rsD", n)])

    def normalize_tail(nrm, hb, qb):
        sg = stg[nrm % NSTG]
        skeys = [("U", "stg", nrm % NSTG, i) for i in range(3)]
        n = nrm

        def y_op(e, sg=sg, n=n, qb=qb, hb=hb):
            ins = None
            for j in range(2):
                ins = e.scalar_tensor_tensor(out=sg[0][:, j, :], in0=sg[1][:, j, :],
                                             scalar=rsD[:, n, j:j + 1], in1=SG[:, 2 * qb + j, hb * 128:(hb + 1) * 128],
                                             op0=ALU.mult, op1=ALU.mult)
            return ins
        A("dve", y_op, reads=[skeys[1], ("rsD", n), ("SG", 2 * qb, hb // 2), ("SG", 2 * qb + 1, hb // 2)],
          writes=[skeys[0]])
        A("dve", TT(Ytok[:, 2 * qb:2 * qb + 2, hb * 128:(hb + 1) * 128], sg[0], bc(CW[:, hb, :], 1, [128, 2, 128]),
                    ALU.mult),
          reads=[skeys[0], "CW"], writes=[("YD", "y", 2 * qb, hb), ("YD", "y", 2 * qb + 1, hb)])

    def pe_warmup(n_mm, psn):
        def warm(e, psn=psn, n_mm=n_mm):
            ins = None
            for k in range(n_mm):
                ins = e.matmul(PS[psn][:, (k % 2) * 512:(k % 2 + 1) * 512], lhsT=QKT[:, 0, 0, 0:128],
                               rhs=QKT[:, 1, 0, 0:512], start=True, stop=True)
            return ins
        A("pe", warm, reads=[("Q", "qkt", 0)], writes=[("P" + psn,)])

    emit_qz(0, "dve")
    emit_qz(1)
    emit_qz(2)
    A("pool", DMA(wout[:, :, :], wout_d[:, :, :], max_dma_last_dim=8192), writes=[("U", "wout")], dma=True, key="c_wout")
    pe_warmup(20, "A")
    emit_qk(0)
    emit_qk(1)
    emit_qk(2)
    nrm = 0
    prev_nrm = None
    for s, (bi, hb, qb, kt) in enumerate(steps):
        ob = nrm % 2
        psOt = PS[OB[ob]][:, :].rearrange("p (c n) -> p c n", c=2)
        okey = ("P" + OB[ob],)
        jmin = max(0, kt - 2 * qb)
        off = jmin * 128
        stn, stb = STB[s % 4]
        psST = bank(stn, stb).rearrange("p (c n) -> p c n", c=2)
        pbi = s % NPT
        A("act", ACTV(PT[pbi][:, :, off:256], psST[:, :, off:256], AF.Exp, scale=0.125),
          writes=[("P" + stn, stb), ("U", "pt", pbi)])
        if s + 3 < len(steps):
            emit_qk(s + 3)

        def mmPV(e, psOt=psOt, pbi=pbi, hb=hb, kt=kt, qb=qb, jmin=jmin):
            ins = None
            for c in range(2):
                for j in range(jmin, 2):
                    ins = e.matmul(psOt[:, c, j * 129:(j + 1) * 129], lhsT=PT[pbi][:, c, j * 128:(j + 1) * 128],
                                   rhs=V[:, kt, hb, :], start=(kt == 0 and j == 0),
                                   stop=(kt == 2 * qb + j), skip_group_check=True)
            return ins
        A("pe", mmPV, reads=[("U", "pt", pbi), ("V", kt, hb // 2)], writes=[okey])
        if kt == 2 * qb + 1:
            if bi + 3 < len(blocks):
                emit_qz(bi + 3)
            normalize_head(nrm, hb, qb, psOt, okey)
            if prev_nrm is not None:
                normalize_tail(*prev_nrm)
            prev_nrm = (nrm, hb, qb)
            nrm += 1

    normalize_tail(*prev_nrm)

    A("sp", lambda e: e.nop(), writes=[("Q",)])

    def emit_tr(t):
        b = t % 2
        pT = bank_bf("C", b)

        def tr8y(e, pT=pT, t=t):
            ins = None
            for c in range(8):
                ins = e.transpose(out=pT[:, c * 128:(c + 1) * 128], in_=Ytok[:, t, c * 128:(c + 1) * 128],
                                  identity=identb[:, :])
            return ins
        A("pe", tr8y, reads=[("YD", "y", t), ("YR", "y", t), "identb"], writes=[("PC", b)])
        A("act", ACTV(ub[b][:, :], pT, AF.Copy), writes=[("PC", b), ("ub", b)])

    def load_xre(t):
        A("sp", DMA(xre[t % NXR], x_d[t * 128:(t + 1) * 128, :]), writes=[("Q", "xre", t % NXR)], dma=True,
          key=("xre", t % NXR))

    for t in range(NXR - 1):
        load_xre(t)
    emit_tr(0)
    for t in range(NT):
        b = t % 2
        if t + NXR - 1 < NT:
            load_xre(t + NXR - 1)
        if t + 1 < NT:
            emit_tr(t + 1)
        yT = ub[b][:, :].rearrange("p (c s) -> p c s", c=8)
        pn = "ABD"[t % 3]
        psY = PS[pn][:, :]
        ykey = ("P" + pn,)

        def mmY(e, psY=psY, yT=yT):
            ins = None
            for nh in range(2):
                for c in range(8):
                    ins = e.matmul(psY[:, nh * 512:(nh + 1) * 512], lhsT=yT[:, c, :], rhs=wout[:, c, nh * 512:(nh + 1) * 512],
                                   start=(c == 0), stop=(c == 7))
            return ins
        A("pe", mmY, reads=[("ub", b), ("U", "wout")], writes=[ykey])
        jt, jkey = junk_full()
        A("act", ACTV(jt[:, :], psY, AF.Square, accum_out=ssO[:, t:t + 1]), writes=[ykey, jkey, ("ssO", t)])
        A("dve", TS(tmO[:, t:t + 1], ssO[:, t:t + 1], 1.0 / D, ALU.mult, EPS, ALU.add), reads=[("ssO", t)],
          writes=[("tmO", t)])
        A("pool", TT(rsO[:, t:t + 1], tmO[:, t:t + 1], MH[:, 0:1], ALU.pow), reads=[("tmO", t), "MH"], writes=[("rsO", t)])
        A("dve", STT(ostg[b], psY, rsO[:, t:t + 1], NW[:, :], ALU.mult, ALU.mult),
          reads=[("rsO", t), "NW"], writes=[ykey, ("Q", "ostg", b)])
        A("dve", TT(ostg[b], ostg[b], xre[t % NXR], ALU.add), reads=[("Q", "ostg", b), ("Q", "xre", t % NXR)],
          writes=[("Q", "ostg", b)])
        A("sp", DMA(out_d[t * 128:(t + 1) * 128, :], ostg[b]), reads=[("Q", "ostg", b)], writes=[("out", t)],
          dma=True, key=("ost", b))
    A("sp", lambda e: None, reads=[("out",)])
    S.emit()
    return nc


_CACHE = {}


def _constants():
    if "c" in _CACHE:
        return _CACHE["c"]
    ident = np.eye(128, dtype=np.float32)
    kk = np.arange(128)
    tri = np.where(kk[None, :] >= kk[:, None], 0.0, -30000.0).astype(np.float32)
    gam = 1.0 - 2.0 ** (-5.0 - np.arange(4, dtype=np.float64))
    m = np.arange(128, dtype=np.float64)
    scale = 128.0 ** -0.5
    maskr = np.zeros((128, 4, 128), np.float64)
    for h in range(4):
        maskr[:, h, :] = (gam[h] ** (-(m[:, None] + 1.0))) * (m[None, :] >= m[:, None]) * scale
    maskr = maskr.reshape(128, 512).astype(np.float32)
    cr = np.zeros((128, 16), np.float64)
    for h in range(4):
        xi = gam[h] ** (m + 1.0)
        cr[:, h] = gam[h] ** (127.0 - m) * scale
        cr[:, 4 + h] = 4.0 * EPS / (xi * xi)
        cr[:, 12 + h] = gam[h] ** 128.0
    cr = cr.astype(np.float32)
    invf = (10000.0 ** (-np.arange(64, dtype=np.float64) / 64.0)).astype(np.float32)
    invf = np.ascontiguousarray(np.broadcast_to(invf[None, :], (128, 64)))
    _CACHE["c"] = dict(ident=ident, tri=tri, maskr=maskr, cr=cr, invf=invf)
    return _CACHE["c"]


def kernel(x, positions, norm_pre_w, w_in, lambda_q1, lambda_k1, lambda_q2, lambda_k2,
           diff_subln_w, w_out, norm_post_w):
    x = np.asarray(x, dtype=np.float32)
    positions = np.asarray(positions, dtype=np.int32)
    w_in0 = np.asarray(w_in, dtype=np.float32)[0]
    w_out0 = np.asarray(w_out, dtype=np.float32)[0]
    B = x.shape[0]
    blocks = []
    for base in (0, 2048):
        for pr in range(2):
            ha, hb = 2 * pr, 2 * pr + 1

            def cs(s, h):
                return np.arange(base + s * 512 + h * 128, base + s * 512 + (h + 1) * 128)
            cols = np.concatenate([cs(0, ha), cs(1, ha), cs(0, hb), cs(1, hb), cs(2, ha), cs(2, hb), cs(3, ha), cs(3, hb)])
            blocks.append(w_in0[:, cols])
    wperm = np.stack(blocks, 0).reshape(4, 8, 128, 1024).transpose(0, 2, 1, 3)
    wperm = np.ascontiguousarray(wperm)
    woutp = np.ascontiguousarray(w_out0.reshape(8, 128, 1024).transpose(1, 0, 2))

    def rep(v, n):
        return np.ascontiguousarray(np.broadcast_to(np.asarray(v, np.float32).reshape(1, n), (128, n)))
    npre = rep(norm_pre_w[0], D)
    npost = rep(norm_post_w[0], D)
    subw = rep(diff_subln_w[0], 128)
    lq = np.stack([rep(lambda_q1[0], 64), rep(lambda_k1[0], 64), rep(lambda_q2[0], 64), rep(lambda_k2[0], 64)], 1)
    lq = np.ascontiguousarray(lq)
    c = _constants()
    in_maps = []
    for b in range(B):
        in_maps.append({
            "x": np.ascontiguousarray(x[b]),
            "pos": np.ascontiguousarray(positions[b].reshape(NT, 128).T),
            "w_in": wperm, "w_out": woutp, "npre": npre, "npost": npost, "subw": subw, "lq": lq,
            "ident": c["ident"], "tri": c["tri"], "maskr": c["maskr"], "invf": c["invf"], "cr": c["cr"],
        })
    nc = build_program()
    res = run_bass_kernel_spmd(nc, in_maps, core_ids=list(range(B)))
    return np.stack([np.asarray(r["out"], dtype=np.float32) for r in res.results], 0)
```

```python
import math
import numpy as np
import concourse.bass as bass
import concourse.mybir as mybir
from concourse.bass_utils import run_bass_kernel_spmd

F32 = mybir.dt.float32
BF16 = mybir.dt.bfloat16
I32 = mybir.dt.int32
AF = mybir.ActivationFunctionType
ALU = mybir.AluOpType
AX = mybir.AxisListType

S_LEN = 2048
D = 1024
NT = 16
EPS = 1e-6
LAMBDA_INIT = 0.8 - 0.6 * math.exp(0.0)
QB = 256
NQB = S_LEN // QB
MAGIC = 12582912.0
TWO_PI = 2.0 * math.pi

STREAMS = ("pe", "act", "dve", "pool", "sp")
EMBED_WAIT = True
RELAXED_SAME_ENGINE = True


class _Op:
    __slots__ = ("idx", "stream", "dma", "fn", "key", "deps", "signaled", "count")


def _k(r):
    return r if isinstance(r, tuple) else (r,)


class _FirstInsWait:
    def __init__(self, eng, sem, val):
        self._e, self._sem, self._val, self._done = eng, sem, val, False

    def __getattr__(self, name):
        f = getattr(self._e, name)

        def call(*a, **k):
            ins = f(*a, **k)
            if not self._done:
                self._done = True
                ins._wait_ge(self._sem, self._val)
            return ins
        return call


class Sched:
    def __init__(self, nc):
        self.nc = nc
        self.ops = []
        self.last_writer = {}
        self.readers = {}
        self.children = {}

    def _related(self, key):
        rel = [key[:i] for i in range(1, len(key))]
        rel.append(key)
        rel.extend(self.children.get(key, ()))
        return rel

    def _register(self, key):
        if key in self.last_writer or key in self.readers:
            return
        for i in range(1, len(key)):
            self.children.setdefault(key[:i], set()).add(key)

    def add(self, stream, fn, reads=(), writes=(), dma=False, key=None):
        op = _Op()
        op.idx = len(self.ops)
        op.stream = stream
        op.dma = dma
        op.fn = fn
        op.key = (key if key is not None else ("dma", op.idx)) if dma else stream
        op.signaled = False
        op.count = 0
        reads = [_k(r) for r in reads]
        writes = [_k(r) for r in writes]
        deps = {}
        for r in reads:
            for q in self._related(r):
                w = self.last_writer.get(q)
                if w is not None:
                    deps[w] = "raw"
        for r in writes:
            pr = stream != "pe" and r[0] in ("PA", "PB", "PC", "PD")
            for q in self._related(r):
                w = self.last_writer.get(q)
                if w is not None and (w not in deps or (deps[w] == "pr" and not pr)):
                    deps[w] = "pr" if pr else "waw"
                for rd in self.readers.get(q, ()):
                    if rd not in deps or (deps[rd] == "pr" and not pr):
                        deps[rd] = "pr" if pr else "war"
        for r in reads:
            self._register(r)
            self.readers.setdefault(r, []).append(op.idx)
        for r in writes:
            self._register(r)
            self.last_writer[r] = op.idx
            self.readers[r] = []
            for q in self.children.get(r, ()):
                self.readers[q] = []
        op.deps = deps
        self.ops.append(op)
        return op

    def emit(self):
        nc = self.nc
        ops = self.ops
        for op in ops:
            keep = {}
            for d, kind in op.deps.items():
                if d == op.idx:
                    continue
                dop = ops[d]
                if not dop.dma and not op.dma and dop.stream == op.stream:
                    if op.stream == "pe" or kind == "pr" or (RELAXED_SAME_ENGINE and kind != "raw"):
                        continue
                if dop.dma:
                    keep[("d", d)] = d
                else:
                    k = ("c", dop.stream)
                    if k not in keep or keep[k] < d:
                        keep[k] = d
            op.deps = sorted(keep.values())
            for d in op.deps:
                ops[d].signaled = True
        counters = {}
        for op in ops:
            if op.signaled:
                counters[op.key] = counters.get(op.key, 0) + (16 if op.dma else 1)
                op.count = counters[op.key]
        sems = {k: nc.alloc_semaphore(name="s%d" % i) for i, k in enumerate(counters.keys())}
        by_stream = {s: [o for o in ops if o.stream == s] for s in STREAMS}

        known = {sn: {} for sn in STREAMS}
        vc = {}
        needs = {}

        def merge(dst, src):
            for k, v in src.items():
                if dst.get(k, 0) < v:
                    dst[k] = v

        for op in ops:
            kn = known[op.stream]
            need = []
            for d in op.deps:
                dop = ops[d]
                if kn.get(dop.key, 0) >= dop.count:
                    continue
                need.append(dop)
                kn[dop.key] = dop.count
                merge(kn, vc[d])
            needs[op.idx] = need
            v = dict(kn)
            if op.signaled:
                v[op.key] = op.count
                if not op.dma:
                    kn[op.key] = max(kn.get(op.key, 0), 0)
            vc[op.idx] = v

        def run_stream(eng, sname):
            for op in by_stream[sname]:
                need = list(needs[op.idx])
                emb = need.pop() if (need and EMBED_WAIT and not op.dma and sname in ("act", "dve", "pool", "pe")
                                     and not getattr(op.fn, "noembed", False)) else None
                for dop in need:
                    eng.wait_ge(sems[dop.key], dop.count)
                if emb is not None:
                    ins = op.fn(_FirstInsWait(eng, sems[emb.key], emb.count))
                else:
                    ins = op.fn(eng)
                if op.signaled:
                    ins.then_inc(sems[op.key], 16 if op.dma else 1)

        with nc.Block() as block:
            @block.sync
            def _(e):
                run_stream(e, "sp")

            @block.scalar
            def _(e):
                run_stream(e, "act")

            @block.vector
            def _(e):
                run_stream(e, "dve")

            @block.gpsimd
            def _(e):
                run_stream(e, "pool")

            @block.tensor
            def _(e):
                run_stream(e, "pe")


def _single(f):
    f.single = True
    return f


def TT(out, in0, in1, op):
    return _single(lambda e: e.tensor_tensor(out=out, in0=in0, in1=in1, op=op))


def TS(out, in0, s1, op0, s2=None, op1=None):
    if op1 is None:
        return _single(lambda e: e.tensor_scalar(out=out, in0=in0, scalar1=s1, scalar2=None, op0=op0))
    return _single(lambda e: e.tensor_scalar(out=out, in0=in0, scalar1=s1, scalar2=s2, op0=op0, op1=op1))


def STT(out, in0, scalar, in1, op0, op1):
    return _single(lambda e: e.scalar_tensor_tensor(out=out, in0=in0, scalar=scalar, in1=in1, op0=op0, op1=op1))


def ACTV(out, in_, func, scale=1.0, accum_out=None):
    if accum_out is None:
        return lambda e: e.activation(out=out, in_=in_, func=func, scale=scale)
    return lambda e: e.activation(out=out, in_=in_, func=func, scale=scale, accum_out=accum_out)


def CP(out, in_):
    return _single(lambda e: e.tensor_copy(out=out, in_=in_))


def DMA(out, in_, **kw):
    return lambda e: e.dma_start(out=out, in_=in_, **kw)


def bc(ap, axis, shape):
    return ap.unsqueeze(axis).broadcast_to(shape)


def build_program():
    nc = bass.Bass("TRN2", target_bir_lowering=False)

    def din(name, shape, dt=F32):
        return nc.dram_tensor(name, shape, dt, kind="ExternalInput").ap()

    x_d = din("x", [S_LEN, D])
    pos_d = din("pos", [128, NT], I32)
    win_d = din("w_in", [4, 128, 8, 1024])
    wout_d = din("w_out", [128, 8, 1024])
    npre_d = din("npre", [128, D])
    npost_d = din("npost", [128, D])
    subw_d = din("subw", [128, 128])
    lq_d = din("lq", [128, 4, 64])
    ident_d = din("ident", [128, 128])
    tri_d = din("tri", [128, 128])
    maskr_d = din("maskr", [128, 512])
    invf_d = din("invf", [128, 64])
    cr_d = din("cr", [128, 16])
    out_d = nc.dram_tensor("out", [S_LEN, D], F32, kind="ExternalOutput").ap()

    def sb(name, shape, dt):
        return nc.alloc_sbuf_tensor("s_" + name, shape, dt)

    U = sb("U", [128, 16384], BF16)
    uT = U[:, :].rearrange("p (c s) -> p c s", c=8)
    wout = U[:, 0:8192].rearrange("p (c n) -> p c n", c=8)
    NPT = 4
    PT = [U[:, 8192 + i * 512: 8192 + (i + 1) * 512].rearrange("p (c q) -> p c q", c=2) for i in range(NPT)]
    NSTG = 2
    stg = [[U[:, 10240 + (b * 3 + i) * 512: 10240 + (b * 3 + i + 1) * 512].bitcast(F32)
            .rearrange("p (j d) -> p j d", j=2) for i in range(3)] for b in range(NSTG)]

    NQZ = 4
    Qz = [U[:, 13312 + i * 512: 13312 + (i + 1) * 512] for i in range(NQZ)]
    Wb = [sb("Wb%d" % i, [128, 8192], BF16) for i in range(2)]
    Wv = [w[:, :].rearrange("p (c n) -> p c n", c=8) for w in Wb]
    W1f = Wb[1][:, :].bitcast(F32)
    ang = W1f[:, 0:1024].rearrange("p (t j) -> p t j", t=NT)
    kf = W1f[:, 1024:2048].rearrange("p (t j) -> p t j", t=NT)
    rr = W1f[:, 2048:3072].rearrange("p (t j) -> p t j", t=NT)
    identf = W1f[:, 3072:3200]
    trif = W1f[:, 3200:3328]

    Q = sb("Q", [128, 16384], BF16)
    QKT = Q[:, :].rearrange("p (a h s) -> p a h s", a=2, h=4)
    Qf = Q[:, :].bitcast(F32)
    ostg = [Qf[:, i * 1024:(i + 1) * 1024] for i in range(2)]
    NXR = 4
    xre = [Qf[:, 2048 + i * 1024: 2048 + (i + 1) * 1024] for i in range(NXR)]

    V = sb("V", [128, NT, 4, 129], BF16)
    SG = sb("SG", [128, NT, 512], BF16)

    Yt = sb("Ytok", [128, NT * 1024], BF16)
    Ytok = Yt[:, :].rearrange("p (t f) -> p t f", t=NT)

    def ydc(t):
        return Ytok[:, t, 0:512]
    QKTc = [ydc(i).rearrange("p (a h s) -> p a h s", a=2, h=2) for i in range(2)]

    def half(k):
        return ydc(2 + k // 2)[:, (k % 2) * 256:(k % 2 + 1) * 256]
    Ktokc = [half(i).rearrange("p (h d) -> p h d", h=2) for i in range(0, 3)]
    Vc = [half(i).rearrange("p (h d) -> p h d", h=2) for i in range(3, 6)]
    SGc = [half(i) for i in (6, 7, 8, 13)]
    Aq = [half(i) for i in range(9, 11)]
    Rbf = [half(i).rearrange("p (h d) -> p h d", h=2) for i in range(11, 13)]
    zc = [ydc(9 + i).bitcast(F32) for i in range(3)]
    NXB = 4
    xb = [Ytok[:, 4 * i:4 * i + 4, 512:1024].bitcast(F32) for i in range(NXB)]

    def x4(ap2d):
        return ap2d.rearrange("p (a b) -> p a b", a=4)

    Ctab = sb("Ctab", [128, NT, 64], F32)
    Stab = sb("Stab", [128, NT, 64], F32)
    NUB = 5
    ub = [sb("ub%d" % i, [128, 1024], BF16) for i in range(NUB)]
    junk = sb("junk", [128, 1024], BF16)
    junk2 = sb("junk2", [128, 1024], BF16)
    jk = {"n": 0}

    def junk_full():
        jk["n"] += 1
        b = jk["n"] % 2
        return (junk, junk2)[b], ("junk", "f", b)

    def junk_128():
        jk["n"] += 1
        k = jk["n"] % 8
        return junk[:, k * 128:(k + 1) * 128], ("junk", "s", k)
    NW = sb("NW", [128, 1024], F32)
    R = sb("R", [128, 2, 128], F32)
    t1 = sb("t1", [128, 512], F32)
    t2 = sb("t2", [128, 512], F32)
    qkrot = [sb("qkrot%d" % i, [128, 512], BF16) for i in range(2)]
    thb = [sb("thb%d" % i, [128, 256], F32) for i in range(2)]
    sgs = sb("sgs", [128, 256], F32)
    identb = sb("identb", [128, 128], BF16)
    trib = sb("trib", [128, 128], BF16)
    maskr = sb("maskr", [128, 512], F32)
    CW = sb("CW", [128, 4, 128], F32)
    subw = sb("subw", [128, 128], F32)
    invf = sb("invf", [128, 64], F32)
    posi = sb("posi", [128, NT], I32)
    posf = sb("posf", [128, NT], F32)
    CR = sb("CR", [128, 16], F32)
    MH = sb("MH", [128, 16], F32)
    LQ = sb("LQ", [128, 4, 64], F32)
    lqp = sb("lqp", [128, 2, 64], F32)
    lst = sb("lst", [128, 8], F32)
    ssA = sb("ssA", [128, NT], F32)
    tmA = sb("tmA", [128, NT], F32)
    rsA = sb("rsA", [128, NT], F32)
    ssR = sb("ssR", [128, 2 * NT, 2], F32)
    tmR = sb("tmR", [128, 2 * NT, 2], F32)
    rsR = sb("rsR", [128, 2 * NT, 2], F32)
    scR = sb("scR", [128, 2 * NT, 2], F32)
    NNRM = NQB * 4
    rlD = sb("rlD", [128, NNRM, 2, 2], F32)
    rl1n = sb("rl1n", [128, NNRM, 2], F32)
    ssD = sb("ssD", [128, NNRM, 2], F32)
    tmD = sb("tmD", [128, NNRM, 2], F32)
    rsD = sb("rsD", [128, NNRM, 2], F32)
    ssO = sb("ssO", [128, NT], F32)
    tmO = sb("tmO", [128, NT], F32)
    rsO = sb("rsO", [128, NT], F32)

    PS = {n: nc.alloc_psum_tensor("ps" + n, [128, 1024], F32) for n in "ABCD"}

    def bank(n, b):
        return PS[n][:, b * 512:(b + 1) * 512]

    def bank_bf(n, b):
        return bank(n, b).bitcast(BF16)

    S = Sched(nc)
    A = S.add

    def load_w(pair, buf):
        A("pool", DMA(Wv[buf][:, :, :], win_d[pair], max_dma_last_dim=8192), writes=[("W", buf)], dma=True, key=("W", buf))

    load_w(0, 0)
    A("sp", DMA(posi[:, :], pos_d[:, :]), writes=["posi"], dma=True, key="c_pos")
    A("sp", DMA(invf[:, :], invf_d[:, :]), writes=["invf"], dma=True, key="c_invf")
    A("sp", DMA(identf, ident_d[:, :]), writes=[("W", 1, "identf")], dma=True, key="c_id")
    A("sp", DMA(xb[0], x4(x_d[0:128, :])), writes=[("YR", "xb", 0)], dma=True, key=("xb", 0))
    A("sp", DMA(NW[:, :], npre_d[:, :]), writes=["NW"], dma=True, key="c_nw")
    for i in range(1, NXB):
        A("sp", DMA(xb[i], x4(x_d[i * 128:(i + 1) * 128, :])), writes=[("YR", "xb", i)], dma=True, key=("xb", i))
    A("sp", DMA(CR[:, :], cr_d[:, :]), writes=["CR"], dma=True, key="c_cr")
    A("sp", DMA(maskr[:, :], maskr_d[:, :]), writes=["maskr"], dma=True, key="c_maskr")
    A("sp", DMA(trif, tri_d[:, :]), writes=[("W", 1, "trif")], dma=True, key="c_tri")
    A("sp", DMA(subw[:, :], subw_d[:, :]), writes=["subw"], dma=True, key="c_subw")
    A("sp", DMA(LQ[:, :, :], lq_d[:, :, :]), writes=["LQ"], dma=True, key="c_lq")

    A("dve", lambda e: e.memset(MH[:, :], -0.5), writes=["MH"])
    A("dve", CP(identb[:, :], identf), reads=[("W", 1, "identf")], writes=["identb"])
    A("dve", CP(posf[:, :], posi[:, :]), reads=["posi"], writes=["posf"])
    A("dve", TT(ang, bc(posf[:, :], 2, [128, NT, 64]), bc(invf[:, :], 1, [128, NT, 64]), ALU.mult),
      reads=["posf", "invf"], writes=[("W", 1, "ang")])
    c1 = 6.28125
    c2 = float(np.float32(TWO_PI - c1))
    c3 = float(TWO_PI - c1 - c2)
    PI_LO = 3.1415925
    PERIOD_LO = float(np.nextafter(np.float32(TWO_PI), np.float32(0.0)))
    A("dve", TS(kf, ang, 1.0 / TWO_PI, ALU.mult, MAGIC, ALU.add), reads=[("W", 1, "ang")], writes=[("W", 1, "kf")])
    A("dve", TS(kf, kf, -MAGIC, ALU.add), reads=[("W", 1, "kf")], writes=[("W", 1, "kf")])
    A("dve", STT(rr, kf, -c1, ang, ALU.mult, ALU.add), reads=[("W", 1, "kf"), ("W", 1, "ang")], writes=[("W", 1, "rr")])
    A("dve", STT(rr, kf, -c2, rr, ALU.mult, ALU.add), reads=[("W", 1, "kf"), ("W", 1, "rr")], writes=[("W", 1, "rr")])
    A("dve", TS(rr, rr, -PI_LO, ALU.max, PI_LO, ALU.min), reads=[("W", 1, "rr")], writes=[("W", 1, "rr")])
    A("act", ACTV(Stab[:, :, :], rr, AF.Sin), reads=[("W", 1, "rr")], writes=["Stab"])
    A("dve", TS(kf, rr, math.pi / 2.0, ALU.add, PI_LO, ALU.is_gt), reads=[("W", 1, "rr")], writes=[("W", 1, "kf")])
    A("dve", STT(kf, kf, -PERIOD_LO, rr, ALU.mult, ALU.add), reads=[("W", 1, "kf"), ("W", 1, "rr")], writes=[("W", 1, "kf")])
    A("act", lambda e: e.activation(out=Ctab[:, :, :], in_=kf, func=AF.Sin, bias=math.pi / 2.0, scale=1.0),
      reads=[("W", 1, "kf")], writes=["Ctab"])
    neglam = lst[:, 5:6]

    def phaseA_s1(t):
        b = t % NUB
        xi = t % NXB
        jt, jkey = junk_full()
        A("act", ACTV(x4(jt[:, :]), xb[xi], AF.Square, accum_out=ssA[:, t:t + 1]),
          reads=[("YR", "xb", xi)], writes=[jkey, ("ssA", t)])
        A("dve", TS(tmA[:, t:t + 1], ssA[:, t:t + 1], 1.0 / D, ALU.mult, EPS, ALU.add),
          reads=[("ssA", t)], writes=[("tmA", t)])
        A("pool", TT(rsA[:, t:t + 1], tmA[:, t:t + 1], MH[:, 0:1], ALU.pow),
          reads=[("tmA", t), "MH"], writes=[("rsA", t)])
        A("dve", STT(x4(ub[b][:, :]), xb[xi], rsA[:, t:t + 1], x4(NW[:, :]), ALU.mult, ALU.mult),
          reads=[("YR", "xb", xi), ("rsA", t), "NW"], writes=[("ub", b)])
        if t + NXB < NT:
            A("sp", DMA(xb[xi], x4(x_d[(t + NXB) * 128:(t + NXB + 1) * 128, :])), writes=[("YR", "xb", xi)],
              dma=True, key=("xb", xi))

    def phaseA_s2(t):
        b = t % NUB
        pb_ = t % 2
        pT = bank_bf("C", pb_)

        def tr8(e, pT=pT, b=b):
            ins = None
            for c in range(8):
                ins = e.transpose(out=pT[:, c * 128:(c + 1) * 128], in_=ub[b][:, c * 128:(c + 1) * 128],
                                  identity=identb[:, :])
            return ins
        A("pe", tr8, reads=[("ub", b), "identb"], writes=[("PC", pb_)])
        A("act", ACTV(uT[:, :, t * 128:(t + 1) * 128], pT.rearrange("p (c s) -> p c s", c=8), AF.Copy),
          writes=[("PC", pb_), ("U", "uT", t)])

    aq = {"s1": 0, "s2": 0}

    def phaseA_s1_step():
        if aq["s1"] < NT:
            phaseA_s1(aq["s1"])
            aq["s1"] += 1

    def phaseA_s2_step():
        if aq["s2"] < aq["s1"]:
            phaseA_s2(aq["s2"])
            aq["s2"] += 1
            if aq["s2"] == NT:
                A("sp", DMA(NW[:, :], npost_d[:, :]), writes=["NW"], dma=True, key="c_nw")

    def phaseA_step():
        phaseA_s2_step()
        phaseA_s1_step()

    for _ in range(NUB):
        phaseA_s1_step()
    phaseA_s2_step()
    phaseA_s1_step()
    phaseA_s2_step()
    A("dve", CP(trib[:, :], trif), reads=[("W", 1, "trif")], writes=["trib"])
    A("dve", TT(lqp[:, 0, :], LQ[:, 0, :], LQ[:, 1, :], ALU.mult), reads=["LQ"], writes=["lqp"])
    A("dve", TT(lqp[:, 1, :], LQ[:, 2, :], LQ[:, 3, :], ALU.mult), reads=["LQ", "lqp"], writes=["lqp"])
    A("dve", lambda e: e.tensor_reduce(out=lst[:, 0:2], in_=lqp[:, :, :], axis=AX.X, op=ALU.add),
      reads=["lqp"], writes=[("lst", 0)])
    A("act", ACTV(lst[:, 2:4], lst[:, 0:2], AF.Exp), reads=[("lst", 0)], writes=[("lst", 1)])
    A("dve", TT(lst[:, 4:5], lst[:, 3:4], lst[:, 2:3], ALU.subtract), reads=[("lst", 1)], writes=[("lst", 2)])
    A("dve", TS(lst[:, 5:6], lst[:, 4:5], -LAMBDA_INIT, ALU.add), reads=[("lst", 2)], writes=[("lst", 3)])
    A("dve", TS(CW[:, :, :], bc(subw[:, :], 1, [128, 4, 128]), (1.0 - LAMBDA_INIT) * 0.5, ALU.mult),
      reads=["subw"], writes=["CW"])

    A("dve", lambda e: e.memset(V[:, :, :, :], 1.0), writes=["V"])

    cnt = {"p": 0}

    def part1(wbuf, t, is_ret, pair, psn):
        i = cnt["p"]
        cnt["p"] += 1
        psP = PS[psn][:, :]
        pkey = ("P" + psn,)
        rb = i % 2
        cb = t % 2

        def mm16(e, psP=psP, wbuf=wbuf, t=t):
            ins = None
            for c in range(8):
                for bk in range(2):
                    ins = e.matmul(psP[:, bk * 512:(bk + 1) * 512], lhsT=uT[:, c, t * 128:(t + 1) * 128],
                                   rhs=Wv[wbuf][:, c, bk * 512:(bk + 1) * 512], start=(c == 0), stop=(c == 7))
            return ins
        A("pe", mm16, reads=[("U", "uT", t), ("W", wbuf)], writes=[pkey])
        gcol = psP[:, 768:1024]
        vcol = psP[:, 512:768].rearrange("p (h d) -> p h d", h=2)
        A("act", ACTV(thb[rb][:, :], gcol, AF.Tanh, scale=0.5), writes=[pkey, ("thb", rb)])
        if is_ret:
            A("act", ACTV(Vc[cb][:, :, :], vcol, AF.Copy), writes=[pkey, ("YD", "Vc", cb)])
            ng, hf = 4, 64
            ct = Ctab[:, t, :]
            st = Stab[:, t, :]
        else:
            A("act", ACTV(V[:, t, 2 * pair:2 * pair + 2, 0:128], vcol, AF.Copy), writes=[pkey, ("V", t, pair)])
            ng, hf = 8, 32
            ct = Ctab[:, t, :].rearrange("p (i two) -> p i two", two=2)[:, :, 0]
            st = Stab[:, t, :].rearrange("p (i two) -> p i two", two=2)[:, :, 0]
        pv = psP[:, 0:512].rearrange("p (g h i) -> p g h i", g=ng, h=2)
        t1v = t1[:, :].rearrange("p (g h i) -> p g h i", g=ng, h=2)
        t2v = t2[:, :].rearrange("p (g h i) -> p g h i", g=ng, h=2)
        cb4 = ct.unsqueeze(1).unsqueeze(1).broadcast_to([128, ng, 2, hf])
        sb3 = st.unsqueeze(1).broadcast_to([128, ng, hf])
        A("dve", TT(t1v, pv, cb4, ALU.mult), reads=["Ctab"], writes=[pkey, "t1"])
        A("dve", STT(t2v[:, :, 0, :], pv[:, :, 1, :], -1.0, sb3, ALU.mult, ALU.mult),
          reads=["Stab"], writes=[pkey, ("t2", 0)])
        A("dve", TT(t2v[:, :, 1, :], pv[:, :, 0, :], sb3, ALU.mult), reads=["Stab"], writes=[pkey, ("t2", 1)])
        if is_ret:
            A("dve", STT(SGc[cb], thb[rb][:, :], 1.0, gcol, ALU.add, ALU.mult),
              reads=[("thb", rb)], writes=[pkey, ("YD", "SGc", cb)])
        else:
            A("dve", STT(SG[:, t, 2 * pair * 128:(2 * pair + 2) * 128], thb[rb][:, :], 1.0, gcol, ALU.add, ALU.mult),
              reads=[("thb", rb)], writes=[pkey, ("SG", t, pair)])
        A("dve", TT(qkrot[rb][:, :], t1[:, :], t2[:, :], ALU.add), reads=["t1", "t2"], writes=[("qkrot", rb)])
        if is_ret:
            for hl in range(2):
                h = 2 * pair + hl
                A("act", ACTV(Ktokc[cb][:, hl, :], qkrot[rb][:, hl * 256 + 128:hl * 256 + 256], AF.Copy,
                              scale=CR[:, h:h + 1]),
                  reads=[("qkrot", rb), "CR"], writes=[("YD", "Ktokc", cb, hl)])
        return (i, rb, cb)

    def part2(t, is_ret, pair, tok):
        i, rb, cb = tok
        slot = i % 2
        pT2 = bank_bf("C", slot)[:, 0:512]
        tkey = ("PC", slot)

        def tr4(e, pT2=pT2, rb=rb):
            ins = None
            for j in range(4):
                ins = e.transpose(out=pT2[:, j * 128:(j + 1) * 128], in_=qkrot[rb][:, j * 128:(j + 1) * 128],
                                  identity=identb[:, :])
            return ins
        A("pe", tr4, reads=[("qkrot", rb), "identb"], writes=[tkey])
        for hl in range(2):
            src = pT2[:, hl * 256:(hl + 1) * 256].rearrange("p (a s) -> p a s", a=2)
            if is_ret:
                A("act", ACTV(QKTc[cb][:, :, hl, :], src, AF.Copy), writes=[tkey, ("YD", "QKTc", cb, hl)])
            else:
                A("act", ACTV(QKT[:, :, 2 * pair + hl, t * 128:(t + 1) * 128], src, AF.Copy),
                  writes=[tkey, ("Q", "qkt", 2 * pair + hl, t)])

    pend = None
    for pair in range(2):
        for t in range(NT):
            tok = part1(pair, t, False, pair, "ABD"[cnt["p"] % 3])
            if pend is not None:
                part2(*pend)
            pend = (t, False, pair, tok)
            phaseA_step()
            if pair == 0 and t == 6:
                load_w(1, 1)
        if pair == 0:
            load_w(2, 0)
    part2(*pend)
    pend = None
    load_w(3, 1)
    while aq["s2"] < NT:
        phaseA_step()

    A("sp", lambda e: e.nop(), writes=[("YD",), ("YR",)])

    SQS = (4.0 / 128.0) ** 0.5

    NCH = 2 * NT

    def ret_iter(i):
        n0, n1, n2, n3 = i, i - 1, i - 2, i - 3
        has0, has1, has2, has3 = (0 <= n0 < NCH), (0 <= n1 < NCH), (0 <= n2 < NCH), (0 <= n3 < NCH)
        psPn = "AB"[i % 2]
        psP = PS[psPn][:, :]
        pk0, pk1 = ("P" + psPn, 0), ("P" + psPn, 1)
        psS = bank("C", 0)[:, 0:256]
        skey = ("PC", 0)
        psKV = bank("C", 1)[:, 0:256]
        kkey = ("PC", 1)
        psO = bank("D", 0)[:, 0:256]
        okey = ("PD", 0)
        pT2 = bank_bf("D", 1)[:, 0:512]
        tkey = ("PD", 1)
        if has2:
            rp2, t2_ = divmod(n2, NT)
            cq2, c32 = n2 % 2, n2 % 3
            rbuf = n2 % 2

            def mmS(e, cq2=cq2):
                ins = None
                for hl in range(2):
                    ins = e.matmul(psS[:, hl * 128:(hl + 1) * 128], lhsT=QKTc[cq2][:, 1, hl, :], rhs=QKTc[cq2][:, 0, hl, :],
                                   start=True, stop=True)
                return ins
            A("pe", mmS, reads=[("YD", "QKTc", cq2)], writes=[skey])
            A("dve", TT(Aq[cq2], psS, maskr[:, 2 * rp2 * 128:(2 * rp2 + 2) * 128], ALU.mult), reads=["maskr"],
              writes=[skey, ("YD", "Aq", cq2)])
        if has0:
            rp0, t0 = divmod(n0, NT)
            rb, c30 = n0 % 2, n0 % 3

            def mm8(bk):
                def f(e, rp=rp0, t=t0, bk=bk):
                    ins = None
                    for c in range(8):
                        ins = e.matmul(psP[:, bk * 512:(bk + 1) * 512], lhsT=uT[:, c, t * 128:(t + 1) * 128],
                                       rhs=Wv[rp][:, c, bk * 512:(bk + 1) * 512], start=(c == 0), stop=(c == 7))
                    return ins
                return f
            A("pe", mm8(0), reads=[("U", "uT", t0), ("W", rp0)], writes=[pk0])
        if has1:
            rb1, cq1 = n1 % 2, n1 % 2

            def tr4(e, rb1=rb1):
                ins = None
                for j in range(4):
                    ins = e.transpose(out=pT2[:, j * 128:(j + 1) * 128], in_=qkrot[rb1][:, j * 128:(j + 1) * 128],
                                      identity=identb[:, :])
                return ins
            A("pe", tr4, reads=[("qkrot", rb1), "identb"], writes=[tkey])
            A("act", ACTV(QKTc[cq1].rearrange("p a h s -> p h a s"),
                          pT2.rearrange("p (h a s) -> p h a s", h=2, a=2), AF.Copy),
              writes=[tkey, ("YD", "QKTc", cq1)])
        if has0:
            A("pe", mm8(1), reads=[("U", "uT", t0), ("W", rp0)], writes=[pk1])
            gcol = psP[:, 768:1024]
            vcol = psP[:, 512:768].rearrange("p (h d) -> p h d", h=2)
            A("act", ACTV(thb[rb][:, :], gcol, AF.Tanh, scale=0.5), writes=[pk1, ("thb", rb)])
            A("act", ACTV(Vc[c30][:, :, :], vcol, AF.Copy), writes=[pk1, ("YD", "Vc", c30)])
        if has2:
            def mmO(e, cq2=cq2, c32=c32, t=t2_, rbuf=rbuf):
                ins = None
                for hl in range(2):
                    o = psO[:, hl * 128:(hl + 1) * 128]
                    ins = e.matmul(o, lhsT=Aq[cq2][:, hl * 128:(hl + 1) * 128], rhs=Vc[c32][:, hl, :],
                                   start=True, stop=(t == 0))
                    if t > 0:
                        ins = e.matmul(o, lhsT=QKTc[cq2][:, 0, hl, :], rhs=Rbf[rbuf][:, hl, :], start=False, stop=True)
                return ins
            A("pe", mmO, reads=[("YD", "Aq", cq2), ("YD", "Vc", c32), ("YD", "QKTc", cq2), ("YD", "Rbf", rbuf)],
              writes=[okey])
            if t2_ < NT - 1:
                def mmKV(e, c32=c32):
                    ins = None
                    for hl in range(2):
                        ins = e.matmul(psKV[:, hl * 128:(hl + 1) * 128], lhsT=Ktokc[c32][:, hl, :], rhs=Vc[c32][:, hl, :],
                                       start=True, stop=True)
                    return ins
                A("pe", mmKV, reads=[("YD", "Ktokc", c32), ("YD", "Vc", c32)], writes=[kkey])
            A("act", ACTV(zc[n2 % 3], psO, AF.Copy), writes=[okey, ("YD", "zc", n2 % 3)])
        if has0:
            ng, hf = 4, 64
            ct = Ctab[:, t0, :]
            st = Stab[:, t0, :]
            pv = psP[:, 0:512].rearrange("p (g h i) -> p g h i", g=ng, h=2)
            t1v = t1[:, :].rearrange("p (g h i) -> p g h i", g=ng, h=2)
            t2v = t2[:, :].rearrange("p (g h i) -> p g h i", g=ng, h=2)
            cb4 = ct.unsqueeze(1).unsqueeze(1).broadcast_to([128, ng, 2, hf])
            sb3 = st.unsqueeze(1).broadcast_to([128, ng, hf])
            A("dve", TT(t1v, pv, cb4, ALU.mult), reads=["Ctab"], writes=[pk0, "t1"])
            A("dve", STT(t2v[:, :, 0, :], pv[:, :, 1, :], -1.0, sb3, ALU.mult, ALU.mult),
              reads=["Stab"], writes=[pk0, ("t2", 0)])
            A("dve", TT(t2v[:, :, 1, :], pv[:, :, 0, :], sb3, ALU.mult), reads=["Stab"], writes=[pk0, ("t2", 1)])
            A("dve", STT(SGc[n0 % 4], thb[rb][:, :], 1.0, gcol, ALU.add, ALU.mult),
              reads=[("thb", rb)], writes=[pk1, ("YD", "SGc", n0 % 4)])
            A("dve", TT(qkrot[rb][:, :], t1[:, :], t2[:, :], ALU.add), reads=["t1", "t2"], writes=[("qkrot", rb)])
        if has2 and t2_ < NT - 1:
            kvv = psKV.rearrange("p (h d) -> p h d", h=2)
            if t2_ == 0:
                A("dve", CP(R[:, :, :], kvv), writes=[kkey, "R"])
            else:
                def rupd(e, kvv=kvv, rp=rp2):
                    ins = None
                    for hl in range(2):
                        h = 2 * rp + hl
                        ins = e.scalar_tensor_tensor(out=R[:, hl, :], in0=R[:, hl, :], scalar=CR[:, 12 + h:13 + h],
                                                     in1=kvv[:, hl, :], op0=ALU.mult, op1=ALU.add)
                    return ins
                A("dve", rupd, reads=["R", "CR"], writes=[kkey, "R"])
            A("pool", CP(Rbf[1 - rbuf][:, :, :], R[:, :, :]), reads=["R"], writes=[("YD", "Rbf", 1 - rbuf)])
        if has0:
            for hl in range(2):
                h = 2 * rp0 + hl
                A("act", ACTV(Ktokc[c30][:, hl, :], qkrot[rb][:, hl * 256 + 128:hl * 256 + 256], AF.Copy,
                              scale=CR[:, h:h + 1]),
                  reads=[("qkrot", rb), "CR"], writes=[("YD", "Ktokc", c30, hl)])
        if has3:
            rp3, t3_ = divmod(n3, NT)
            c0 = 512 + 2 * rp3 * 128

            def ytail(e, n3=n3, t3_=t3_, c0=c0):
                ins = None
                for hl in range(2):
                    ins = e.scalar_tensor_tensor(out=Ytok[:, t3_, c0 + hl * 128:c0 + (hl + 1) * 128],
                                                 in0=zc[n3 % 3][:, hl * 128:(hl + 1) * 128], scalar=rsR[:, n3, hl:hl + 1],
                                                 in1=SGc[n3 % 4][:, hl * 128:(hl + 1) * 128], op0=ALU.mult, op1=ALU.mult)
                return ins
            A("dve", ytail, reads=[("rsR", n3), ("YD", "SGc", n3 % 4), ("YD", "zc", n3 % 3)], writes=[("YR", "y", t3_, rp3)])
        if has2:
            for hl in range(2):
                js, jkey = junk_128()
                A("act", ACTV(js, zc[n2 % 3][:, hl * 128:(hl + 1) * 128], AF.Square, scale=SQS,
                              accum_out=ssR[:, n2, hl:hl + 1]),
                  reads=[("YD", "zc", n2 % 3)], writes=[jkey, ("ssR", n2, hl)])
            A("dve", TT(tmR[:, n2, :], ssR[:, n2, :], CR[:, 4 + 2 * rp2:6 + 2 * rp2], ALU.add), reads=[("ssR", n2), "CR"],
              writes=[("tmR", n2)])
            A("pool", TT(rsR[:, n2, :], tmR[:, n2, :], MH[:, 0:2], ALU.pow), reads=[("tmR", n2), "MH"],
              writes=[("rsR", n2)])

    for i in range(NCH + 3):
        ret_iter(i)

    A("sp", lambda e: e.nop(), writes=[("U",), ("YD",)])
    A("pool", DMA(wout[:, :, :], wout_d[:, :, :], max_dma_last_dim=8192), writes=[("U", "wout")], dma=True, key="c_wout")

    blocks = [(hb, qb) for hb in range(4) for qb in range(NQB)]
    steps = []
    for bi, (hb, qb) in enumerate(blocks):
        for kt in range(2 * qb + 2):
            steps.append((bi, hb, qb, kt))
    OB = ("A", "B")
    STB = (("C", 0), ("C", 1), ("D", 0), ("D", 1))
    for i in range(NQZ):
        A("dve", lambda e, i=i: e.memset(Qz[i], 0.0), writes=[("U", "qz", i)])

    def emit_qz(bi):
        hb, qb = blocks[bi]
        z = Qz[bi % NQZ]
        A("pool", CP(z[0:64, 0:256], QKT[0:64, 0, hb, qb * QB:(qb + 1) * QB]), reads=[("Q", "qkt", hb)],
          writes=[("U", "qz", bi % NQZ, 0)])
        A("pool", CP(z[64:128, 256:512], QKT[64:128, 0, hb, qb * QB:(qb + 1) * QB]), reads=[("Q", "qkt", hb)],
          writes=[("U", "qz", bi % NQZ, 1)])

    def emit_qk(s):
        bi, hb, qb, kt = steps[s]
        off = max(0, kt - 2 * qb) * 128
        stn, stb = STB[s % 4]
        psST = bank(stn, stb)
        z = Qz[bi % NQZ]

        diag = kt >= 2 * qb

        def mmQK(e, psST=psST, hb=hb, kt=kt, off=off, z=z, diag=diag):
            kT = QKT[:, 1, hb, kt * 128:(kt + 1) * 128]
            if not diag:
                return e.matmul(psST[:, 0:512], lhsT=kT, rhs=z[:, 0:512], start=True, stop=True)
            if off == 0:
                e.matmul(psST[:, 0:512], lhsT=kT, rhs=z[:, 0:512], start=True, stop=False, skip_group_check=True)
                e.matmul(psST[:, 0:128], lhsT=identb[:, :], rhs=trib[:, :], start=False, stop=False,
                         skip_group_check=True)
                return e.matmul(psST[:, 256:384], lhsT=identb[:, :], rhs=trib[:, :], start=False, stop=True,
                                skip_group_check=True)
            e.matmul(psST[:, off:256], lhsT=kT, rhs=z[:, off:256], start=True, stop=False, skip_group_check=True)
            e.matmul(psST[:, off:256], lhsT=identb[:, :], rhs=trib[:, :], start=False, stop=False,
                     skip_group_check=True)
            e.matmul(psST[:, 256 + off:512], lhsT=kT, rhs=z[:, 256 + off:512], start=False, stop=False,
                     skip_group_check=True)
            return e.matmul(psST[:, 256 + off:512], lhsT=identb[:, :], rhs=trib[:, :], start=False, stop=True,
                            skip_group_check=True)
        A("pe", mmQK, reads=[("Q", "qkt", hb), ("U", "qz", bi % NQZ), "identb", "trib"], writes=[("P" + stn, stb)])

    def normalize_head(nrm, hb, qb, psOt, okey):
        sg = stg[nrm % NSTG]
        skeys = [("U", "stg", nrm % NSTG, i) for i in range(3)]
        n = nrm
        o4 = psOt[:, :, 0:258].rearrange("p c (j k) -> p c j k", k=129)
        A("dve", lambda e, o4=o4, n=n: e.reciprocal(out=rlD[:, n, :, :], in_=o4[:, :, :, 128]),
          writes=[okey, ("rlD", n)])
        A("dve", TS(rl1n[:, n, :], rlD[:, n, 1, :], neglam, ALU.mult), reads=[("rlD", n), ("lst", 3)],
          writes=[("rl1n", n)])
        A("dve", TT(sg[0], o4[:, 0, :, 0:128], bc(rlD[:, n, 0, :], 2, [128, 2, 128]), ALU.mult),
          reads=[("rlD", n)], writes=[okey, skeys[0]])
        def ya_op(e, o4=o4, sg=sg, n=n):
            ins = None
            for j in range(2):
                ins = e.scalar_tensor_tensor(out=sg[1][:, j, :], in0=o4[:, 1, j, 0:128], scalar=rl1n[:, n, j:j + 1],
                                             in1=sg[0][:, j, :], op0=ALU.mult, op1=ALU.add)
            return ins
        A("dve", ya_op, reads=[("rl1n", n), skeys[0]], writes=[okey, skeys[1]])
        A("dve", TT(sg[2], sg[1], sg[1], ALU.mult), reads=[skeys[1]], writes=[skeys[2]])
        A("dve", lambda e, sg=sg, n=n: e.tensor_reduce(out=ssD[:, n, :], in_=sg[2], axis=AX.X, op=ALU.add),
          reads=[skeys[2]], writes=[("ssD", n)])
        A("dve", TS(tmD[:, n, :], ssD[:, n, :], 1.0 / 128.0, ALU.mult, EPS, ALU.add), reads=[("ssD", n)],
          writes=[("tmD", n)])
        A("pool", TT(rsD[:, n, :], tmD[:, n, :], MH[:, 0:2], ALU.pow), reads=[("tmD", n), "MH"], writes=[("rsD", n)])

    def normalize_tail(nrm, hb, qb):
        sg = stg[nrm % NSTG]
        skeys = [("U", "stg", nrm % NSTG, i) for i in range(3)]
        n = nrm

        def y_op(e, sg=sg, n=n, qb=qb, hb=hb):
            ins = None
            for j in range(2):
                ins = e.scalar_tensor_tensor(out=sg[0][:, j, :], in0=sg[1][:, j, :],
                                             scalar=rsD[:, n, j:j + 1], in1=SG[:, 2 * qb + j, hb * 128:(hb + 1) * 128],
                                             op0=ALU.mult, op1=ALU.mult)
            return ins
        A("dve", y_op, reads=[skeys[1], ("rsD", n), ("SG", 2 * qb, hb // 2), ("SG", 2 * qb + 1, hb // 2)],
          writes=[skeys[0]])
        A("dve", TT(Ytok[:, 2 * qb:2 * qb + 2, hb * 128:(hb + 1) * 128], sg[0], bc(CW[:, hb, :], 1, [128, 2, 128]),
                    ALU.mult),
          reads=[skeys[0], "CW"], writes=[("YD", "y", 2 * qb, hb), ("YD", "y", 2 * qb + 1, hb)])

    def pe_warmup(n_mm, psn):
        def warm(e, psn=psn, n_mm=n_mm):
            ins = None
            for k in range(n_mm):
                ins = e.matmul(PS[psn][:, (k % 2) * 512:(k % 2 + 1) * 512], lhsT=QKT[:, 0, 0, 0:128],
                               rhs=QKT[:, 1, 0, 0:512], start=True, stop=True)
            return ins
        A("pe", warm, reads=[("Q", "qkt", 0)], writes=[("P" + psn,)])

    emit_qz(0)
    emit_qz(1)
    emit_qz(2)
    pe_warmup(20, "A")
    emit_qk(0)
    emit_qk(1)
    emit_qk(2)
    nrm = 0
    prev_nrm = None
    for s, (bi, hb, qb, kt) in enumerate(steps):
        ob = nrm % 2
        psOt = PS[OB[ob]][:, :].rearrange("p (c n) -> p c n", c=2)
        okey = ("P" + OB[ob],)
        jmin = max(0, kt - 2 * qb)
        off = jmin * 128
        stn, stb = STB[s % 4]
        psST = bank(stn, stb).rearrange("p (c n) -> p c n", c=2)
        pbi = s % NPT
        A("act", ACTV(PT[pbi][:, :, off:256], psST[:, :, off:256], AF.Exp, scale=0.125),
          writes=[("P" + stn, stb), ("U", "pt", pbi)])
        if s + 3 < len(steps):
            emit_qk(s + 3)

        def mmPV(e, psOt=psOt, pbi=pbi, hb=hb, kt=kt, qb=qb, jmin=jmin):
            ins = None
            for c in range(2):
                for j in range(jmin, 2):
                    ins = e.matmul(psOt[:, c, j * 129:(j + 1) * 129], lhsT=PT[pbi][:, c, j * 128:(j + 1) * 128],
                                   rhs=V[:, kt, hb, :], start=(kt == 0 and j == 0),
                                   stop=(kt == 2 * qb + j), skip_group_check=True)
            return ins
        A("pe", mmPV, reads=[("U", "pt", pbi), ("V", kt, hb // 2)], writes=[okey])
        if kt == 2 * qb + 1:
            if bi + 3 < len(blocks):
                emit_qz(bi + 3)
            normalize_head(nrm, hb, qb, psOt, okey)
            if prev_nrm is not None:
                normalize_tail(*prev_nrm)
            prev_nrm = (nrm, hb, qb)
            nrm += 1

    normalize_tail(*prev_nrm)

    A("sp", lambda e: e.nop(), writes=[("Q",)])

    def emit_tr(t):
        b = t % 2
        pT = bank_bf("C", b)

        def tr8y(e, pT=pT, t=t):
            ins = None
            for c in range(8):
                ins = e.transpose(out=pT[:, c * 128:(c + 1) * 128], in_=Ytok[:, t, c * 128:(c + 1) * 128],
                                  identity=identb[:, :])
            return ins
        A("pe", tr8y, reads=[("YD", "y", t), ("YR", "y", t), "identb"], writes=[("PC", b)])
        A("act", ACTV(ub[b][:, :], pT, AF.Copy), writes=[("PC", b), ("ub", b)])

    def load_xre(t):
        A("sp", DMA(xre[t % NXR], x_d[t * 128:(t + 1) * 128, :]), writes=[("Q", "xre", t % NXR)], dma=True,
          key=("xre", t % NXR))

    for t in range(NXR - 1):
        load_xre(t)
    emit_tr(0)
    for t in range(NT):
        b = t % 2
        if t + NXR - 1 < NT:
            load_xre(t + NXR - 1)
        if t + 1 < NT:
            emit_tr(t + 1)
        yT = ub[b][:, :].rearrange("p (c s) -> p c s", c=8)
        pn = "ABD"[t % 3]
        psY = PS[pn][:, :]
        ykey = ("P" + pn,)

        def mmY(e, psY=psY, yT=yT):
            ins = None
            for nh in range(2):
                for c in range(8):
                    ins = e.matmul(psY[:, nh * 512:(nh + 1) * 512], lhsT=yT[:, c, :], rhs=wout[:, c, nh * 512:(nh + 1) * 512],
                                   start=(c == 0), stop=(c == 7))
            return ins
        A("pe", mmY, reads=[("ub", b), ("U", "wout")], writes=[ykey])
        jt, jkey = junk_full()
        A("act", ACTV(jt[:, :], psY, AF.Square, accum_out=ssO[:, t:t + 1]), writes=[ykey, jkey, ("ssO", t)])
        A("dve", TS(tmO[:, t:t + 1], ssO[:, t:t + 1], 1.0 / D, ALU.mult, EPS, ALU.add), reads=[("ssO", t)],
          writes=[("tmO", t)])
        A("pool", TT(rsO[:, t:t + 1], tmO[:, t:t + 1], MH[:, 0:1], ALU.pow), reads=[("tmO", t), "MH"], writes=[("rsO", t)])
        A("dve", STT(ostg[b], psY, rsO[:, t:t + 1], NW[:, :], ALU.mult, ALU.mult),
          reads=[("rsO", t), "NW"], writes=[ykey, ("Q", "ostg", b)])
        A("dve", TT(ostg[b], ostg[b], xre[t % NXR], ALU.add), reads=[("Q", "ostg", b), ("Q", "xre", t % NXR)],
          writes=[("Q", "ostg", b)])
        A("sp", DMA(out_d[t * 128:(t + 1) * 128, :], ostg[b]), reads=[("Q", "ostg", b)], writes=[("out", t)],
          dma=True, key=("ost", b))
    A("sp", lambda e: None, reads=[("out",)])
    S.emit()
    return nc


_CACHE = {}


def _constants():
    if "c" in _CACHE:
        return _CACHE["c"]
    ident = np.eye(128, dtype=np.float32)
    kk = np.arange(128)
    tri = np.where(kk[None, :] >= kk[:, None], 0.0, -30000.0).astype(np.float32)
    gam = 1.0 - 2.0 ** (-5.0 - np.arange(4, dtype=np.float64))
    m = np.arange(128, dtype=np.float64)
    scale = 128.0 ** -0.5
    maskr = np.zeros((128, 4, 128), np.float64)
    for h in range(4):
        maskr[:, h, :] = (gam[h] ** (-(m[:, None] + 1.0))) * (m[None, :] >= m[:, None]) * scale
    maskr = maskr.reshape(128, 512).astype(np.float32)
    cr = np.zeros((128, 16), np.float64)
    for h in range(4):
        xi = gam[h] ** (m + 1.0)
        cr[:, h] = gam[h] ** (127.0 - m) * scale
        cr[:, 4 + h] = 4.0 * EPS / (xi * xi)
        cr[:, 12 + h] = gam[h] ** 128.0
    cr = cr.astype(np.float32)
    invf = (10000.0 ** (-np.arange(64, dtype=np.float64) / 64.0)).astype(np.float32)
    invf = np.ascontiguousarray(np.broadcast_to(invf[None, :], (128, 64)))
    _CACHE["c"] = dict(ident=ident, tri=tri, maskr=maskr, cr=cr, invf=invf)
    return _CACHE["c"]


def kernel(x, positions, norm_pre_w, w_in, lambda_q1, lambda_k1, lambda_q2, lambda_k2,
           diff_subln_w, w_out, norm_post_w):
    x = np.asarray(x, dtype=np.float32)
    positions = np.asarray(positions, dtype=np.int32)
    w_in0 = np.asarray(w_in, dtype=np.float32)[0]
    w_out0 = np.asarray(w_out, dtype=np.float32)[0]
    B = x.shape[0]
    blocks = []
    for base in (0, 2048):
        for pr in range(2):
            ha, hb = 2 * pr, 2 * pr + 1

            def cs(s, h):
                return np.arange(base + s * 512 + h * 128, base + s * 512 + (h + 1) * 128)
            cols = np.concatenate([cs(0, ha), cs(1, ha), cs(0, hb), cs(1, hb), cs(2, ha), cs(2, hb), cs(3, ha), cs(3, hb)])
            blocks.append(w_in0[:, cols])
    wperm = np.stack(blocks, 0).reshape(4, 8, 128, 1024).transpose(0, 2, 1, 3)
    wperm = np.ascontiguousarray(wperm)
    woutp = np.ascontiguousarray(w_out0.reshape(8, 128, 1024).transpose(1, 0, 2))

    def rep(v, n):
        return np.ascontiguousarray(np.broadcast_to(np.asarray(v, np.float32).reshape(1, n), (128, n)))
    npre = rep(norm_pre_w[0], D)
    npost = rep(norm_post_w[0], D)
    subw = rep(diff_subln_w[0], 128)
    lq = np.stack([rep(lambda_q1[0], 64), rep(lambda_k1[0], 64), rep(lambda_q2[0], 64), rep(lambda_k2[0], 64)], 1)
    lq = np.ascontiguousarray(lq)
    c = _constants()
    in_maps = []
    for b in range(B):
        in_maps.append({
            "x": np.ascontiguousarray(x[b]),
            "pos": np.ascontiguousarray(positions[b].reshape(NT, 128).T),
            "w_in": wperm, "w_out": woutp, "npre": npre, "npost": npost, "subw": subw, "lq": lq,
            "ident": c["ident"], "tri": c["tri"], "maskr": c["maskr"], "invf": c["invf"], "cr": c["cr"],
        })
    nc = build_program()
    res = run_bass_kernel_spmd(nc, in_maps, core_ids=list(range(B)))
    return np.stack([np.asarray(r["out"], dtype=np.float32) for r in res.results], 0)
```
